# Optimizing a Trainium2 kernel written in Bass

```python
import math
import jax, jax.numpy as jnp
from jax import lax
import numpy as np

D_MODEL = 1024
BATCH = 4
SEQ = 8192
DEPTH = 2

CHUNK = 64
QBLOCK = 128
EPS = 1e-6
NEG_INF = -1e30

LRU_WIDTH = 512
LRU_HEADS = 8
LRU_HEAD_DIM = LRU_WIDTH // LRU_HEADS
CONV_WIDTH = 4
LRU_C = 8.0

MLA_HEADS = 8
MLA_Q_LORA = 384
MLA_KV_LORA = 256
MLA_NOPE = 64
MLA_ROPE = 32
MLA_V = 64
ROPE_BASE = 10000.0

FOX_HEADS = 8
FOX_HEAD_DIM = 64
FOX_WIDTH = FOX_HEADS * FOX_HEAD_DIM

N_BRANCH = 3
D_FF = ((8 * D_MODEL // 3 + 255) // 256) * 256
PLE_DIM = 256

SPLIT_SIZES = (
    LRU_WIDTH,
    LRU_WIDTH,
    MLA_Q_LORA,
    MLA_KV_LORA + MLA_ROPE,
    FOX_WIDTH,
    FOX_WIDTH,
    FOX_WIDTH,
    FOX_HEADS,
    N_BRANCH * D_MODEL,
)
D_IN = 2 * LRU_WIDTH + MLA_Q_LORA + MLA_KV_LORA + MLA_ROPE + 3 * FOX_WIDTH + FOX_HEADS + N_BRANCH * D_MODEL

kernel_name = "hybrid_gated_rglru_mla_fox_encoder"


def rmsnorm(x, g):
    xf = x.astype(jnp.float32)
    y = xf * lax.rsqrt(jnp.mean(xf * xf, axis=-1, keepdims=True) + EPS)
    return (y * g.astype(jnp.float32)).astype(x.dtype)


def split_columns(z):
    idx = []
    acc = 0
    for s in SPLIT_SIZES[:-1]:
        acc += s
        idx.append(acc)
    return jnp.split(z, idx, axis=-1)


def rope(x, cos, sin):
    half = x.shape[-1] // 2
    x1, x2 = x[..., :half], x[..., half:]
    c = cos[None, :, None, :].astype(x.dtype)
    s = sin[None, :, None, :].astype(x.dtype)
    return jnp.concatenate([x1 * c - x2 * s, x2 * c + x1 * s], axis=-1)


def block_attention(q, k, v, scale, unit, decay=None):
    B, S, H, Dk = q.shape
    nb = S // QBLOCK
    q_blocks = q.reshape(B, nb, QBLOCK, H, Dk).transpose(1, 0, 2, 3, 4)
    key_unit = jnp.arange(S) // unit
    decay_t = None if decay is None else decay.transpose(0, 2, 1)

    def one_block(args):
        ib, q_blk = args
        s = jnp.einsum('bqhd,bkhd->bhqk', q_blk, k, preferred_element_type=jnp.float32) * scale
        if decay_t is not None:
            dq = lax.dynamic_slice_in_dim(decay_t, ib * QBLOCK, QBLOCK, axis=2)
            s = s + dq[:, :, :, None] - decay_t[:, :, None, :]
        q_unit = (ib * QBLOCK + jnp.arange(QBLOCK)) // unit
        mask = q_unit[:, None] >= key_unit[None, :]
        s = jnp.where(mask[None, None], s, NEG_INF)
        pr = jax.nn.softmax(s, axis=-1)
        return jnp.einsum('bhqk,bkhd->bqhd', pr.astype(v.dtype), v)

    out = lax.map(one_block, (jnp.arange(nb), q_blocks))
    return out.transpose(1, 0, 2, 3, 4).reshape(B, S, H, v.shape[-1])


def _lru_combine(left, right):
    a1, b1 = left
    a2, b2 = right
    return a1 * a2, a2 * b1 + b2


def rglru_branch(u, u_gate, conv_w, conv_b, wa, ba, wx, bx, lam):
    B, S, W = u.shape
    up = jnp.pad(u, ((0, 0), (CONV_WIDTH - 1, 0), (0, 0)))
    xc = conv_b + up[:, 0:S] * conv_w[0]
    for kk in range(1, CONV_WIDTH):
        xc = xc + up[:, kk:kk + S] * conv_w[kk]
    xh = xc.reshape(B, S, LRU_HEADS, LRU_HEAD_DIM)
    r = jax.nn.sigmoid(jnp.einsum('bshi,hij->bshj', xh, wa).reshape(B, S, W) + ba)
    ig = jax.nn.sigmoid(jnp.einsum('bshi,hij->bshj', xh, wx).reshape(B, S, W) + bx)
    log_a = -LRU_C * r.astype(jnp.float32) * jax.nn.softplus(-lam.astype(jnp.float32))
    a = jnp.exp(log_a)
    b = jnp.sqrt(-jnp.expm1(2.0 * log_a)) * (ig * xc).astype(jnp.float32)
    _, h = lax.associative_scan(_lru_combine, (a, b), axis=1)
    return h.astype(u.dtype) * jax.nn.gelu(u_gate)


def mla_branch(c_q, ckv_rope, q_norm, wuq, kv_norm, wukv, cos, sin):
    B, S, _ = c_q.shape
    q = (rmsnorm(c_q, q_norm) @ wuq).reshape(B, S, MLA_HEADS, MLA_NOPE + MLA_ROPE)
    q_nope, q_rope = q[..., :MLA_NOPE], q[..., MLA_NOPE:]
    c_kv, k_rope = ckv_rope[..., :MLA_KV_LORA], ckv_rope[..., MLA_KV_LORA:]
    kv = (rmsnorm(c_kv, kv_norm) @ wukv).reshape(B, S, MLA_HEADS, MLA_NOPE + MLA_V)
    k_nope, v = kv[..., :MLA_NOPE], kv[..., MLA_NOPE:]
    q_rope = rope(q_rope, cos, sin)
    k_rope = rope(k_rope[:, :, None, :], cos, sin)
    q_full = jnp.concatenate([q_nope, q_rope], axis=-1)
    k_full = jnp.concatenate([k_nope, jnp.broadcast_to(k_rope, (B, S, MLA_HEADS, MLA_ROPE))], axis=-1)
    o = block_attention(q_full, k_full, v, (MLA_NOPE + MLA_ROPE) ** -0.5, CHUNK)
    return o.reshape(B, S, MLA_HEADS * MLA_V)


def fox_branch(fq, fk, fv, f_logit, bf):
    B, S, _ = fq.shape
    q = fq.reshape(B, S, FOX_HEADS, FOX_HEAD_DIM)
    k = fk.reshape(B, S, FOX_HEADS, FOX_HEAD_DIM)
    v = fv.reshape(B, S, FOX_HEADS, FOX_HEAD_DIM)
    log_f = jax.nn.log_sigmoid((f_logit + bf).astype(jnp.float32))
    cum = jnp.cumsum(log_f, axis=1)
    o = block_attention(q, k, v, FOX_HEAD_DIM ** -0.5, 1, decay=cum)
    return o.reshape(B, S, FOX_WIDTH)


def setup_inputs(seed: int = 0) -> dict:
    key = jax.random.key(seed)
    ks = jax.random.split(key, 32)

    def nrm(k, shape, scale):
        return jax.random.normal(k, shape, jnp.float32) * scale

    def gain(k, shape):
        return 1.0 + 0.05 * jax.random.normal(k, shape, jnp.float32)

    u = jax.random.uniform(ks[10], (DEPTH, LRU_WIDTH), jnp.float32, 0.9, 0.999)
    a = u ** (1.0 / LRU_C)
    lru_lambda = jnp.log(a) - jnp.log1p(-a)

    return {
        "x": nrm(ks[0], (BATCH, SEQ, D_MODEL), 1.0),
        "p": nrm(ks[1], (DEPTH, BATCH, SEQ, PLE_DIM), 1.0),
        "mix_norm": gain(ks[2], (DEPTH, D_MODEL)),
        "w_in": nrm(ks[3], (DEPTH, D_MODEL, D_IN), D_MODEL ** -0.5),
        "gate_b": nrm(ks[4], (DEPTH, N_BRANCH * D_MODEL), 0.1),
        "conv_w": nrm(ks[5], (DEPTH, CONV_WIDTH, LRU_WIDTH), CONV_WIDTH ** -0.5),
        "conv_b": nrm(ks[6], (DEPTH, LRU_WIDTH), 0.1),
        "lru_wa": nrm(ks[7], (DEPTH, LRU_HEADS, LRU_HEAD_DIM, LRU_HEAD_DIM), LRU_HEAD_DIM ** -0.5),
        "lru_ba": nrm(ks[8], (DEPTH, LRU_WIDTH), 0.1),
        "lru_wx": nrm(ks[9], (DEPTH, LRU_HEADS, LRU_HEAD_DIM, LRU_HEAD_DIM), LRU_HEAD_DIM ** -0.5),
        "lru_bx": nrm(ks[11], (DEPTH, LRU_WIDTH), 0.1),
        "lru_lambda": lru_lambda,
        "mla_q_norm": gain(ks[12], (DEPTH, MLA_Q_LORA)),
        "mla_wuq": nrm(ks[13], (DEPTH, MLA_Q_LORA, MLA_HEADS * (MLA_NOPE + MLA_ROPE)), MLA_Q_LORA ** -0.5),
        "mla_kv_norm": gain(ks[14], (DEPTH, MLA_KV_LORA)),
        "mla_wukv": nrm(ks[15], (DEPTH, MLA_KV_LORA, MLA_HEADS * (MLA_NOPE + MLA_V)), MLA_KV_LORA ** -0.5),
        "fox_bf": jax.random.uniform(ks[16], (DEPTH, FOX_HEADS), jnp.float32, 1.0, 5.0),
        "w_br_a": nrm(ks[17], (DEPTH, LRU_WIDTH, D_MODEL), LRU_WIDTH ** -0.5),
        "w_br_b": nrm(ks[18], (DEPTH, MLA_HEADS * MLA_V, D_MODEL), (MLA_HEADS * MLA_V) ** -0.5),
        "w_br_c": nrm(ks[19], (DEPTH, FOX_WIDTH, D_MODEL), FOX_WIDTH ** -0.5),
        "w_o": nrm(ks[20], (DEPTH, D_MODEL, D_MODEL), D_MODEL ** -0.5),
        "ffn_norm": gain(ks[21], (DEPTH, D_MODEL)),
        "w_gate_up": nrm(ks[22], (DEPTH, D_MODEL, 2 * D_FF), D_MODEL ** -0.5),
        "w_down": nrm(ks[23], (DEPTH, D_FF, D_MODEL), D_FF ** -0.5),
        "ple_norm": gain(ks[24], (DEPTH, D_MODEL)),
        "w_ple_gate": nrm(ks[25], (DEPTH, D_MODEL, D_MODEL), D_MODEL ** -0.5),
        "w_ple": nrm(ks[26], (DEPTH, PLE_DIM, D_MODEL), PLE_DIM ** -0.5),
        "final_norm": gain(ks[27], (D_MODEL,)),
    }


def reference(x, p, mix_norm, w_in, gate_b, conv_w, conv_b, lru_wa, lru_ba, lru_wx, lru_bx, lru_lambda,
              mla_q_norm, mla_wuq, mla_kv_norm, mla_wukv, fox_bf, w_br_a, w_br_b, w_br_c, w_o,
              ffn_norm, w_gate_up, w_down, ple_norm, w_ple_gate, w_ple, final_norm):
    B, S, D = x.shape
    pos = jnp.arange(S, dtype=jnp.float32)
    inv_freq = ROPE_BASE ** (-jnp.arange(0, MLA_ROPE, 2, dtype=jnp.float32) / MLA_ROPE)
    ang = pos[:, None] * inv_freq[None, :]
    cos, sin = jnp.cos(ang), jnp.sin(ang)

    for i in range(DEPTH):
        h = rmsnorm(x, mix_norm[i])
        z = h @ w_in[i]
        u_rnn, u_gelu, c_q, ckv_rope, fq, fk, fv, f_logit, gate_logit = split_columns(z)
        y_a = rglru_branch(u_rnn, u_gelu, conv_w[i], conv_b[i], lru_wa[i], lru_ba[i],
                           lru_wx[i], lru_bx[i], lru_lambda[i]) @ w_br_a[i]
        y_b = mla_branch(c_q, ckv_rope, mla_q_norm[i], mla_wuq[i], mla_kv_norm[i],
                         mla_wukv[i], cos, sin) @ w_br_b[i]
        y_c = fox_branch(fq, fk, fv, f_logit, fox_bf[i]) @ w_br_c[i]
        g = jax.nn.sigmoid(gate_logit + gate_b[i]).reshape(B, S, N_BRANCH, D)
        merged = g[:, :, 0] * y_a + g[:, :, 1] * y_b + g[:, :, 2] * y_c
        x = x + merged @ w_o[i]
        hf = rmsnorm(x, ffn_norm[i]) @ w_gate_up[i]
        x = x + (jax.nn.silu(hf[..., :D_FF]) * hf[..., D_FF:]) @ w_down[i]
        pg = jax.nn.sigmoid(rmsnorm(x, ple_norm[i]) @ w_ple_gate[i])
        x = x + pg * (p[i] @ w_ple[i])
    return rmsnorm(x, final_norm)
```

```python
from contextlib import ExitStack
import numpy as np
import ml_dtypes
import concourse.bass as bass
import concourse.mybir as mybir
from concourse.bass_utils import run_bass_kernel_spmd

F32 = mybir.dt.float32
BF16 = mybir.dt.bfloat16
AF = mybir.ActivationFunctionType
ALU = mybir.AluOpType
AX = mybir.AxisListType

ENGS = ["pe", "act", "dve", "pool", "sp"]
EMBED_WAIT = True
BLOCKNAME = {"pe": "tensor", "act": "scalar", "dve": "vector", "pool": "gpsimd", "sp": "sync"}


class Sched:
    def __init__(self, nc, same_engine_sync=True):
        self.nc = nc
        self.ops = []
        self.lastw = {}
        self.readers = {}
        self.chan_count = {}
        self.chan_inc = {}
        self.barrier_deps = set()
        self.last_on_eng = {}
        self.last_on_chan = {}
        self.same_engine_sync = same_engine_sync
        self.seg = 0

    def add(self, eng, fn, reads=(), writes=(), dma=False, chan=None, inc=16):
        oid = len(self.ops)
        deps = set(self.barrier_deps)
        for r in reads:
            w = self.lastw.get(r)
            if w is not None:
                deps.add(w)
        for w_ in writes:
            w = self.lastw.get(w_)
            if w is not None:
                deps.add(w)
            deps.update(self.readers.get(w_, ()))
        deps = {(self.last_on_chan[self.ops[d]["chan"]] if self.ops[d]["dma"] else d) for d in deps}
        for r in reads:
            self.readers.setdefault(r, []).append(oid)
        for w_ in writes:
            self.lastw[w_] = oid
            self.readers[w_] = []
        op = dict(eng=eng, fn=fn, deps=deps, dma=dma, chan=chan, needed=False, sig=None, seg=self.seg)
        if dma:
            assert chan is not None
            c = self.chan_count.get(chan, 0) + 1
            self.chan_count[chan] = c
            op["sig"] = (("dma", chan), inc * c)
            op["inc"] = inc
            self.chan_inc[chan] = inc
            self.last_on_chan[chan] = oid
        else:
            self.last_on_eng[eng] = oid
        self.ops.append(op)
        return oid

    def barrier(self):
        self.seg += 1
        self.barrier_deps = set(self.last_on_eng.values()) | set(self.last_on_chan.values())

    def finalize(self):
        ops = self.ops
        for op in ops:
            pruned = {}
            for d in op["deps"]:
                dop = ops[d]
                key = ("dma", dop["chan"]) if dop["dma"] else ("eng", dop["eng"])
                if key not in pruned or d > pruned[key]:
                    pruned[key] = d
            pd = []
            for key, d in pruned.items():
                if key[0] == "eng" and key[1] == op["eng"] and not op["dma"]:
                    if op["eng"] == "pe" or not self.same_engine_sync:
                        continue
                pd.append(d)
            op["pdeps"] = pd
            op["deps"] = None
            for d in pd:
                ops[d]["needed"] = True
        cnt = {e: 0 for e in ENGS}
        for op in ops:
            if not op["dma"] and op["needed"]:
                cnt[op["eng"]] += 1
                op["sig"] = (("eng", op["eng"]), cnt[op["eng"]])
        self.final_counts = cnt

    def emit(self):
        nc = self.nc
        self.finalize()
        ops = self.ops
        with ExitStack() as st:
            sems = {}
            for e in ENGS:
                sems[("eng", e)] = st.enter_context(nc.semaphore(f"s_{e}"))
            for i, ch in enumerate(self.chan_count):
                sems[("dma", ch)] = st.enter_context(nc.semaphore(f"d{i}"))
            seen_all = {e: {} for e in ENGS}
            nseg = self.seg + 1
            for sg in range(nseg):
                segops = [op for op in ops if op["seg"] == sg]
                if not segops and sg != nseg - 1:
                    continue
                with nc.Block() as block:
                    for e in ENGS:
                        oplist = [op for op in segops if op["eng"] == e]
                        last = (sg == nseg - 1)

                        def body(eh, oplist=oplist, e=e, last=last):
                            seen = seen_all[e]
                            for op in oplist:
                                need = []
                                for d in op["pdeps"]:
                                    key, val = ops[d]["sig"]
                                    if seen.get(key, 0) < val:
                                        need.append((key, val))
                                        seen[key] = val
                                emb = need.pop() if (need and EMBED_WAIT and not op["dma"]) else None
                                for key, val in need:
                                    eh.wait_ge(sems[key], val)
                                ins = op["fn"](eh)
                                if emb is not None:
                                    ins._wait_ge(sems[emb[0]], emb[1])
                                if op["dma"]:
                                    ins.then_inc(sems[op["sig"][0]], op["inc"])
                                elif op["needed"]:
                                    ins.then_inc(sems[op["sig"][0]], 1)
                            if e == "sp" and last:
                                for ch, c in self.chan_count.items():
                                    key = ("dma", ch)
                                    if seen.get(key, 0) < self.chan_inc[ch] * c:
                                        eh.wait_ge(sems[key], self.chan_inc[ch] * c)

                        if oplist or (e == "sp" and last):
                            getattr(block, BLOCKNAME[e])(body)


EPS = 1e-6
NTOK = 4096
TT = 512
C_URNN, C_UGELU, C_CQ, C_CKV, C_FQ, C_FK, C_FV, C_FL, C_GATE = 0, 512, 1024, 1408, 1696, 2208, 2720, 3232, 3240
D_IN = 6312


def rmsnorm_T(S, nc, es, x_src, gcol, hT, pfx, ntok=NTOK, D=1024):
    KC = D // 128
    xt = [es.enter_context(nc.sbuf_tensor(f"{pfx}_xt{i}", [128, KC, TT], F32)) for i in range(2)]
    sq = es.enter_context(nc.sbuf_tensor(f"{pfx}_sq", [128, KC, TT], BF16))
    rstd = es.enter_context(nc.sbuf_tensor(f"{pfx}_rstd", [128, TT], F32))
    onesm = es.enter_context(nc.sbuf_tensor(f"{pfx}_ones", [128, 128], BF16))
    ps = es.enter_context(nc.psum_tensor(f"{pfx}_ps", [128, TT], F32))
    S.add("pool", lambda e: e.memset(onesm[:], 1.0 / D), writes=[(pfx, "ones")])
    xv = x_src.rearrange("(k p) n -> p k n", p=128)
    for t in range(ntok // TT):
        b = t % 2
        S.add("sp", lambda e, b=b, t=t: e.dma_start(out=xt[b][:], in_=xv[:, :, t * TT:(t + 1) * TT]),
              writes=[(pfx, "xt", b)], dma=True, chan=(pfx, "xt", b))
        S.add("act", lambda e, b=b: e.activation(out=sq[:], in_=xt[b][:], func=AF.Square),
              reads=[(pfx, "xt", b)], writes=[(pfx, "sq")])
        for k in range(KC):
            S.add("pe", lambda e, k=k: e.matmul(ps[:], lhsT=onesm[:], rhs=sq[:, k, :], start=(k == 0), stop=(k == KC - 1)),
                  reads=[(pfx, "sq"), (pfx, "ones")], writes=[(pfx, "ps")])
        S.add("act", lambda e: e.activation(out=rstd[:], in_=ps[:], func=AF.Sqrt, bias=EPS, scale=1.0),
              reads=[(pfx, "ps")], writes=[(pfx, "rstd")])
        S.add("dve", lambda e: e.reciprocal(out=rstd[:], in_=rstd[:]),
              reads=[(pfx, "rstd")], writes=[(pfx, "rstd")])
        for k in range(KC):
            S.add("dve", lambda e, k=k, b=b, t=t: e.scalar_tensor_tensor(
                out=hT[:, k, t * TT:(t + 1) * TT], in0=xt[b][:, k, :], scalar=gcol[:, k:k + 1], in1=rstd[:],
                op0=ALU.mult, op1=ALU.mult),
                reads=[(pfx, "xt", b), (pfx, "rstd"), "gcols"], writes=[("hT", t)])


def linear_T(S, nc, es, pfx, w_src, K, cols, rhs_fn, rhs_res_fn, ntok, evac_fn, wdt=BF16, GW=512):
    KC = K // 128
    wv = w_src.rearrange("(k p) m -> p k m", p=128)
    groups = []
    cur = []
    for ci, (c0, n) in enumerate(cols):
        if cur and (cur[0][1] + GW < c0 + n or cur[-1][1] + cur[-1][2] != c0):
            groups.append(cur)
            cur = []
        cur.append((ci, c0, n))
    if cur:
        groups.append(cur)
    wb = [es.enter_context(nc.sbuf_tensor(f"{pfx}_w{i}", [128, KC, GW], wdt)) for i in range(2)]
    pss = [es.enter_context(nc.psum_tensor(f"{pfx}_ps{i}", [128, TT], F32)) for i in range(2)]
    pi = 0
    for gi, g in enumerate(groups):
        b = gi % 2
        g0 = g[0][1]
        gn = g[-1][1] + g[-1][2] - g0
        S.add("pool", lambda e, b=b, g0=g0, gn=gn: e.dma_start(out=wb[b][:, :, 0:gn], in_=wv[:, :, g0:g0 + gn]),
              writes=[(pfx, "w", b)], dma=True, chan=(pfx, "w", b))
        for (ci, c0, n) in g:
            for t in range(ntok // TT):
                p = pi % 2
                pi += 1
                for k in range(KC):
                    S.add("pe", lambda e, b=b, p=p, k=k, t=t, c0=c0, n=n, g0=g0: e.matmul(
                        pss[p][0:n, :], lhsT=wb[b][:, k, c0 - g0:c0 - g0 + n], rhs=rhs_fn(k, t),
                        start=(k == 0), stop=(k == KC - 1)),
                        reads=[(pfx, "w", b)] + rhs_res_fn(k, t), writes=[(pfx, "ps", p)])
                evac_fn(ci, t, pss[p], (pfx, "ps", p), n)


def phase1(S, nc, xres, w_in_l, mixg_l, outs):
    with ExitStack() as es:
        hT = es.enter_context(nc.sbuf_tensor("p1_hT", [128, 8, NTOK], BF16))
        gcol = es.enter_context(nc.sbuf_tensor("p1_gcol", [128, 8], F32))
        S.add("sp", lambda e: e.dma_start(out=gcol[:], in_=mixg_l.rearrange("(k p) -> p k", p=128), allow_slow_non_contiguous=True),
              writes=["gcols"], dma=True, chan="gcols")
        with ExitStack() as es2:
            rmsnorm_T(S, nc, es2, xres, gcol, hT, "p1n")
        S.barrier()
        stage32 = [es.enter_context(nc.sbuf_tensor(f"p1_st32_{i}", [128, NTOK], F32)) for i in range(2)]
        stage16 = [es.enter_context(nc.sbuf_tensor(f"p1_st16_{i}", [128, NTOK], BF16)) for i in range(2)]
        tiles = []
        for j in range(8):
            tiles.append((j * 128, 128, outs["uT"], j * 128, F32))
        for j in range(3):
            tiles.append((C_CQ + j * 128, 128, outs["cqT"], j * 128, F32))
        for j in range(2):
            tiles.append((C_CKV + j * 128, 128, outs["ckvT"], j * 128, F32))
        tiles.append((C_CKV + 256, 32, outs["ckvT"], 256, F32))
        for j in range(4):
            tiles.append((C_FQ + j * 128, 128, outs["fqT"], j * 128, BF16))
        for j in range(4):
            tiles.append((C_FK + j * 128, 128, outs["fkT"], j * 128, BF16))
        tiles.append((C_FL, 8, outs["flT"], 0, F32))
        cols = [(c0, n) for (c0, n, _, _, _) in tiles]
        nT = NTOK // TT

        def evac(ci, t, ps, ps_res, rows):
            c0, n, dst, r0, dt = tiles[ci]
            sb_i = ci % 2
            st = stage32[sb_i] if dt == F32 else stage16[sb_i]
            view = st[0:n, t * TT:(t + 1) * TT]
            eng = "act" if (t % 2 == 0) else "dve"
            if eng == "act":
                S.add("act", lambda e: e.copy(out=view, in_=ps[0:n, :]), reads=[ps_res], writes=[("p1st", dt == F32, sb_i, t)])
            else:
                S.add("dve", lambda e: e.tensor_copy(out=view, in_=ps[0:n, :]), reads=[ps_res], writes=[("p1st", dt == F32, sb_i, t)])
            if t == nT - 1:
                src = st[0:n, :]
                S.add("sp", lambda e: e.dma_start(out=dst[r0:r0 + n, :], in_=src),
                      reads=[("p1st", dt == F32, sb_i, tt) for tt in range(nT)], writes=[("p1out", ci)], dma=True, chan=("p1st", dt == F32, sb_i))

        linear_T(S, nc, es, "p1l", w_in_l, 1024, cols, lambda k, t: hT[:, k, t * TT:(t + 1) * TT],
                 lambda k, t: [("hT", t)], NTOK, evac)
        wfv = es.enter_context(nc.sbuf_tensor("p1_wfv", [128, 8, 512], BF16))
        wv = w_in_l.rearrange("(k p) m -> p k m", p=128)
        S.add("pool", lambda e: e.dma_start(out=wfv[:], in_=wv[:, :, C_FV:C_FV + 512]), writes=["wfv"], dma=True, chan="wfv")
        psv = [es.enter_context(nc.psum_tensor(f"p1_psv{i}", [128, 512], F32)) for i in range(2)]
        stv = [es.enter_context(nc.sbuf_tensor(f"p1_stv{i}", [128, 4, 512], BF16)) for i in range(2)]
        fvv = outs["fv"].rearrange("(b p) c -> p b c", p=128)
        for tb in range(NTOK // 128):
            p = tb % 2
            sgi = (tb // 4) % 2
            for k in range(8):
                S.add("pe", lambda e, p=p, k=k, tb=tb: e.matmul(psv[p][:], lhsT=hT[:, k, tb * 128:(tb + 1) * 128], rhs=wfv[:, k, :],
                                                                 start=(k == 0), stop=(k == 7)),
                      reads=["wfv", ("hT", tb // 4)], writes=[("psv", p)])
            if tb % 2 == 0:
                S.add("act", lambda e, p=p, sgi=sgi, tb=tb: e.copy(out=stv[sgi][:, tb % 4, :], in_=psv[p][:]),
                      reads=[("psv", p)], writes=[("stv", sgi, tb % 4)])
            else:
                S.add("dve", lambda e, p=p, sgi=sgi, tb=tb: e.tensor_copy(out=stv[sgi][:, tb % 4, :], in_=psv[p][:]),
                      reads=[("psv", p)], writes=[("stv", sgi, tb % 4)])
            if tb % 4 == 3:
                b0 = tb - 3
                S.add("sp", lambda e, sgi=sgi, b0=b0: e.dma_start(out=fvv[:, b0:b0 + 4, :], in_=stv[sgi][:]),
                      reads=[("stv", sgi, q) for q in range(4)], writes=[("fvout", tb)], dma=True, chan=("stv", sgi))
    S.barrier()


ST = 1024
NTT = ST // TT
D_FF = 2816


class PsPool:
    def __init__(self, S, nc, es, n, pfx):
        self.S = S
        self.t = [es.enter_context(nc.psum_tensor(f"{pfx}_ps{i}", [128, TT], F32)) for i in range(n)]
        self.pfx = pfx
        self.i = 0

    def group(self, rows, mms, ncols=TT):
        i = self.i % len(self.t)
        self.i += 1
        ps = self.t[i]
        res = (self.pfx, "ps", i)
        n = len(mms)
        for q, (l, r, rd) in enumerate(mms):
            self.S.add("pe", lambda e, l=l, r=r, q=q: e.matmul(ps[0:rows, 0:ncols], lhsT=l, rhs=r, start=(q == 0), stop=(q == n - 1)),
                       reads=list(rd), writes=[res])
        return ps[0:rows, 0:ncols], res


def norm_sb(S, nc, P, src_fn, src_res_fn, gcol, gres, dst_fn, dst_res_fn, sq, rstd, onesm, nt, KC=8):
    for t in range(nt):
        for k in range(KC):
            S.add("act", lambda e, k=k, t=t: e.activation(out=sq[:, k, :], in_=src_fn(k, t), func=AF.Square),
                  reads=[src_res_fn(k, t)], writes=[("sq", k)])
        ps, pres = P.group(128, [(onesm[:], sq[:, k, :], [("sq", k), "ones"]) for k in range(KC)])
        S.add("act", lambda e, ps=ps: e.activation(out=rstd[:], in_=ps, func=AF.Sqrt, bias=EPS, scale=1.0),
              reads=[pres], writes=["rstd"])
        S.add("dve", lambda e: e.reciprocal(out=rstd[:], in_=rstd[:]), reads=["rstd"], writes=["rstd"])
        for k in range(KC):
            S.add("dve", lambda e, k=k, t=t: e.scalar_tensor_tensor(
                out=dst_fn(k, t), in0=src_fn(k, t), scalar=gcol[:, k:k + 1], in1=rstd[:], op0=ALU.mult, op1=ALU.mult),
                reads=[src_res_fn(k, t), "rstd", gres], writes=[dst_res_fn(k, t)])


def phase3(S, nc, xres_in, xres_out, brT, pT_l, W, final_out=None, stop=9, wq="pool"):
    with ExitStack() as es:
        sb = lambda name, shape, dt: es.enter_context(nc.sbuf_tensor("p3_" + name, shape, dt))
        x1T = sb("x1T", [128, 8, ST], F32)
        sq = sb("sq", [128, 8, TT], BF16)
        rstd = sb("rstd", [128, TT], F32)
        onesm = sb("ones", [128, 128], BF16)
        hT = sb("hT", [128, 8, ST], BF16)
        brS = sb("brS", [128, 3, 4, ST], BF16)
        mgT = sb("mgT", [128, 8, ST], BF16)
        actT = sb("actT", [128, 11, ST], BF16)
        pS = sb("pS", [128, 2, ST], BF16)
        wsm = [sb(f"wsm{i}", [128, 3, 12, 128], BF16) for i in range(2)]
        wst = [sb(f"wst{i}", [128, 11, 512], BF16) for i in range(2)]
        gs = [sb(f"gs{i}", [128, TT], F32) for i in range(3)]
        tm = [sb(f"tm{i}", [128, TT], F32) for i in range(3)]
        vec = sb("vec", [128, 24 + 8 * 4], F32)
        P = PsPool(S, nc, es, 7, "p3")

        S.add("pool", lambda e: e.memset(onesm[:], 1.0 / 1024), writes=["ones"])
        S.add("sp", lambda e: e.dma_start(out=vec[:, 0:24], in_=W["gate_b"].rearrange("(j p) -> p j", p=128), allow_slow_non_contiguous=True),
              writes=["vec"], dma=True, chan="vec")
        for i, nm in enumerate(["mixg", "ffng", "pleg", "fing"]):
            if nm in W:
                S.add("sp", lambda e, i=i, nm=nm: e.dma_start(out=vec[:, 24 + 8 * i:32 + 8 * i], in_=W[nm].rearrange("(k p) -> p k", p=128),
                                                               allow_slow_non_contiguous=True), writes=["vec"], dma=True, chan="vec")
        gb = vec[:, 0:24]
        gmix, gffn, gple, gfin = (vec[:, 24 + 8 * i:32 + 8 * i] for i in range(4))
        xin_v = xres_in.rearrange("(k p) n -> p k n", p=128)
        pv = pT_l.rearrange("(k p) n -> p k n", p=128)
        wsi = 0
        wti = 0

        def x1res(k, t):
            return ("x1", k, t)

        for s in range(NTOK // ST):
            tok0 = s * ST
            for t in range(NTT):
                S.add("sp", lambda e, t=t, tok0=tok0: e.dma_start(out=x1T[:, :, t * TT:(t + 1) * TT], in_=xin_v[:, :, tok0 + t * TT:tok0 + (t + 1) * TT]),
                      writes=[x1res(k, t) for k in range(8)], dma=True, chan=("x1ld", t))
            for j in range(3):
                S.add("sp", lambda e, j=j, tok0=tok0: e.dma_start(out=brS[:, j, :, :], in_=brT[j].rearrange("(k p) n -> p k n", p=128)[:, :, tok0:tok0 + ST]),
                      writes=[("br", j)], dma=True, chan=("br", j))
            S.add("pool", lambda e, tok0=tok0: e.dma_start(out=pS[:], in_=pv[:, :, tok0:tok0 + ST]), writes=["pS"], dma=True, chan="pS")
            norm_sb(S, nc, P, lambda k, t: x1T[:, k, t * TT:(t + 1) * TT], x1res, gmix, "vec",
                    lambda k, t: hT[:, k, t * TT:(t + 1) * TT], lambda k, t: ("hT", k, t), sq, rstd, onesm, NTT)
            for m in range(8):
                b = wsi % 2
                wsi += 1
                for j in range(3):
                    S.add(wq, lambda e, b=b, j=j, m=m: e.dma_start(
                        out=wsm[b][:, j, 0:4, :], in_=W["w_br"][j].rearrange("(k p) c -> p k c", p=128)[:, :, m * 128:(m + 1) * 128]),
                        writes=[("wsm", b, j, 0)], dma=True, chan=("wsm", b))
                    gc = (0 if "w_gate" in W else C_GATE) + j * 1024 + m * 128
                    S.add(wq, lambda e, b=b, j=j, gc=gc: e.dma_start(
                        out=wsm[b][:, j, 4:12, :], in_=W.get("w_gate", W["w_in"]).rearrange("(k p) c -> p k c", p=128)[:, :, gc:gc + 128]),
                        writes=[("wsm", b, j, 1)], dma=True, chan=("wsm", b))
                for t in range(NTT):
                    tsl = slice(t * TT, (t + 1) * TT)
                    for j in range(3):
                        psy, ry = P.group(128, [(wsm[b][:, j, k, :], brS[:, j, k, tsl], [("wsm", b, j, 0), ("br", j)]) for k in range(4)])
                        psg, rg = P.group(128, [(wsm[b][:, j, 4 + k, :], hT[:, k, tsl], [("wsm", b, j, 1), ("hT", k, t)]) for k in range(8)])
                        S.add("act", lambda e, j=j, m=m, psg=psg: e.activation(out=gs[j][:], in_=psg, func=AF.Sigmoid, bias=gb[:, j * 8 + m:j * 8 + m + 1], scale=1.0),
                              reads=[rg, "vec"], writes=[("gs", j)])
                        S.add("dve", lambda e, j=j, psy=psy: e.tensor_tensor(out=tm[j][:], in0=psy, in1=gs[j][:], op=ALU.mult),
                              reads=[ry, ("gs", j)], writes=[("tm", j)])
                    S.add("pool", lambda e: e.tensor_tensor(out=tm[0][:], in0=tm[0][:], in1=tm[1][:], op=ALU.add),
                          reads=[("tm", 0), ("tm", 1)], writes=[("tm", 0)])
                    S.add("pool", lambda e, m=m, tsl=tsl: e.tensor_tensor(out=mgT[:, m, tsl], in0=tm[0][:], in1=tm[2][:], op=ALU.add),
                          reads=[("tm", 0), ("tm", 2)], writes=[("mg", m, t)])

            def lin(wsrc, KC, ncols, rhs_fn, rhs_res_fn, evac, row0=0):
                nonlocal wti
                wv_ = wsrc.rearrange("(k p) c -> p k c", p=128)
                k0 = row0 // 128
                for g0 in range(0, ncols, 512):
                    gn = min(512, ncols - g0)
                    b_ = wti % 2
                    wti += 1
                    S.add(wq, lambda e, b_=b_, g0=g0, gn=gn: e.dma_start(out=wst[b_][:, 0:KC, 0:gn], in_=wv_[:, k0:k0 + KC, g0:g0 + gn]),
                          writes=[("wst", b_)], dma=True, chan=("wst", b_))
                    for c in range(gn // 128):
                        m_ = (g0 // 128) + c
                        for t in range(NTT):
                            ps, r = P.group(128, [(wst[b_][:, k, c * 128:(c + 1) * 128], rhs_fn(k, t), [("wst", b_), rhs_res_fn(k, t)]) for k in range(KC)])
                            evac(m_, t, ps, r)

            def add_into_x1(m_, t, ps, r):
                tsl = slice(t * TT, (t + 1) * TT)
                S.add("dve", lambda e: e.tensor_tensor(out=x1T[:, m_, tsl], in0=ps, in1=x1T[:, m_, tsl], op=ALU.add),
                      reads=[r, x1res(m_, t)], writes=[x1res(m_, t)])

            if stop >= 1:
                lin(W["w_o"], 8, 1024, lambda k, t: mgT[:, k, t * TT:(t + 1) * TT], lambda k, t: ("mg", k, t), add_into_x1)
            if stop >= 2:
                norm_sb(S, nc, P, lambda k, t: x1T[:, k, t * TT:(t + 1) * TT], x1res, gffn, "vec",
                        lambda k, t: hT[:, k, t * TT:(t + 1) * TT], lambda k, t: ("hT", k, t), sq, rstd, onesm, NTT)
                gu_v = W["w_gu"].rearrange("(k p) c -> p k c", p=128)
                for half in range(2):
                    h0 = half * (D_FF // 2)
                    for g0 in range(0, D_FF // 2, 256):
                        gn = min(256, D_FF // 2 - g0)
                        b_ = wti % 2
                        wti += 1
                        S.add(wq, lambda e, b_=b_, g0=g0, gn=gn, h0=h0: e.dma_start(out=wst[b_][:, 0:8, 0:gn], in_=gu_v[:, :, h0 + g0:h0 + g0 + gn]),
                              writes=[("wst", b_), ("wst", b_, 0)], dma=True, chan=("wst", b_))
                        S.add(wq, lambda e, b_=b_, g0=g0, gn=gn, h0=h0: e.dma_start(out=wst[b_][:, 0:8, 256:256 + gn], in_=gu_v[:, :, D_FF + h0 + g0:D_FF + h0 + g0 + gn]),
                              writes=[("wst", b_, 1)], dma=True, chan=("wst", b_))
                        for c in range(gn // 128):
                            ci = g0 // 128 + c
                            for t in range(NTT):
                                tsl = slice(t * TT, (t + 1) * TT)
                                psg, rg = P.group(128, [(wst[b_][:, k, c * 128:(c + 1) * 128], hT[:, k, tsl], [("wst", b_), ("wst", b_, 0), ("hT", k, t)]) for k in range(8)])
                                psu, ru = P.group(128, [(wst[b_][:, k, 256 + c * 128:256 + (c + 1) * 128], hT[:, k, tsl], [("wst", b_), ("wst", b_, 1), ("hT", k, t)]) for k in range(8)])
                                q = (ci * NTT + t) % 3
                                S.add("act", lambda e, q=q, psg=psg: e.activation(out=gs[q][:], in_=psg, func=AF.Silu), reads=[rg], writes=[("gs", q)])
                                S.add("dve", lambda e, q=q, psu=psu, ci=ci, tsl=tsl: e.tensor_tensor(out=actT[:, ci, tsl], in0=psu, in1=gs[q][:], op=ALU.mult),
                                      reads=[ru, ("gs", q)], writes=[("act", ci, t)])
                    lin(W["w_down"], 11, 1024, lambda k, t: actT[:, k, t * TT:(t + 1) * TT], lambda k, t: ("act", k, t), add_into_x1, row0=h0)
            if stop >= 3:
                norm_sb(S, nc, P, lambda k, t: x1T[:, k, t * TT:(t + 1) * TT], x1res, gple, "vec",
                        lambda k, t: hT[:, k, t * TT:(t + 1) * TT], lambda k, t: ("hT", k, t), sq, rstd, onesm, NTT)
                pg_v = W["w_pg"].rearrange("(k p) c -> p k c", p=128)
                pe_v = W["w_ple"].rearrange("(k p) c -> p k c", p=128)
                for g0 in range(0, 1024, 256):
                    b_ = wti % 2
                    wti += 1
                    S.add(wq, lambda e, b_=b_, g0=g0: e.dma_start(out=wst[b_][:, 0:8, 0:256], in_=pg_v[:, :, g0:g0 + 256]),
                          writes=[("wst", b_), ("wst", b_, 0)], dma=True, chan=("wst", b_))
                    S.add(wq, lambda e, b_=b_, g0=g0: e.dma_start(out=wst[b_][:, 0:2, 256:512], in_=pe_v[:, :, g0:g0 + 256]),
                          writes=[("wst", b_, 1)], dma=True, chan=("wst", b_))
                    for c in range(2):
                        m_ = g0 // 128 + c
                        for t in range(NTT):
                            tsl = slice(t * TT, (t + 1) * TT)
                            psg, rg = P.group(128, [(wst[b_][:, k, c * 128:(c + 1) * 128], hT[:, k, tsl], [("wst", b_), ("wst", b_, 0), ("hT", k, t)]) for k in range(8)])
                            pse, re_ = P.group(128, [(wst[b_][:, k, 256 + c * 128:256 + (c + 1) * 128], pS[:, k, tsl], [("wst", b_), ("wst", b_, 1), "pS"]) for k in range(2)])
                            q = (m_ * NTT + t) % 3
                            S.add("act", lambda e, q=q, psg=psg: e.activation(out=gs[q][:], in_=psg, func=AF.Sigmoid), reads=[rg], writes=[("gs", q)])
                            S.add("dve", lambda e, q=q, pse=pse: e.tensor_tensor(out=tm[q][:], in0=pse, in1=gs[q][:], op=ALU.mult),
                                  reads=[re_, ("gs", q)], writes=[("tm", q)])
                            S.add("dve", lambda e, q=q, m_=m_, tsl=tsl: e.tensor_tensor(out=x1T[:, m_, tsl], in0=tm[q][:], in1=x1T[:, m_, tsl], op=ALU.add),
                                  reads=[("tm", q), x1res(m_, t)], writes=[x1res(m_, t)])
            if final_out is None:
                xo_v = xres_out.rearrange("(k p) n -> p k n", p=128)
                for t in range(NTT):
                    S.add("sp", lambda e, t=t, tok0=tok0: e.dma_start(out=xo_v[:, :, tok0 + t * TT:tok0 + (t + 1) * TT], in_=x1T[:, :, t * TT:(t + 1) * TT]),
                          reads=[x1res(k, t) for k in range(8)], writes=[("xout", s, t)], dma=True, chan=("x1ld", t))
            else:
                fo_v = final_out.rearrange("(k p) n -> p k n", p=128)
                norm_sb(S, nc, P, lambda k, t: x1T[:, k, t * TT:(t + 1) * TT], x1res, gfin, "vec",
                        lambda k, t: x1T[:, k, t * TT:(t + 1) * TT], x1res, sq, rstd, onesm, NTT)
                for t in range(NTT):
                    S.add("sp", lambda e, t=t, tok0=tok0: e.dma_start(out=fo_v[:, :, tok0 + t * TT:tok0 + (t + 1) * TT], in_=x1T[:, :, t * TT:(t + 1) * TT]),
                          reads=[x1res(k, t) for k in range(8)], writes=[("xout", s, t)], dma=True, chan=("x1ld", t))
    S.barrier()


SEQ = 8192
HALF = 4096
NKB = SEQ // 128
NEG = -30000.0
TL = 2048


def hsl(ap3, r0, r1, t0, n):
    h = t0 // HALF
    assert (t0 + n - 1) // HALF == h
    return ap3[h, r0:r1, t0 - h * HALF:t0 - h * HALF + n]


def phase2(S, nc, I, W, C, brT_out, scr, hook_c=None, hook_mid=None, hook_d=None):
    IR = I.get("res", {})
    with ExitStack() as es0:
        psum = [es0.enter_context(nc.psum_tensor(f"p2_ps{i}", [128, TT], F32)) for i in range(8)]
        identf = es0.enter_context(nc.sbuf_tensor("p2_identf", [128, 128], F32))
        identb = es0.enter_context(nc.sbuf_tensor("p2_identb", [128, 128], BF16))
        rampF = es0.enter_context(nc.sbuf_tensor("p2_rampF", [128, 896], BF16))
        rampM = es0.enter_context(nc.sbuf_tensor("p2_rampM", [128, 896], BF16))
        ncumT = es0.enter_context(nc.sbuf_tensor("p2_ncumT", [128, NKB * 4], F32))
        S.add("sp", lambda e: e.dma_start(out=identf[:], in_=C["ident"]), writes=["identf"], dma=True, chan="identf")
        S.add("pool", lambda e: e.dma_start(out=identb[:], in_=C["ident"]), writes=["identb"], dma=True, chan="identb")
        S.add("pool", lambda e: e.dma_start(out=rampF[:], in_=C["ramp_fox"]), writes=["rampF"], dma=True, chan="rampF")
        S.add("pool", lambda e: e.dma_start(out=rampM[:], in_=C["ramp_mla"]), writes=["rampM"], dma=True, chan="rampM")

        with ExitStack() as es:
            sb = lambda n, s, d: es.enter_context(nc.sbuf_tensor("p2a_" + n, s, d))
            fl = sb("fl", [4, SEQ], F32)
            lg = sb("lg", [4, SEQ], F32)
            onesr = sb("onesr", [4, SEQ], F32)
            ncum = sb("ncum", [4, SEQ], F32)
            qa = sb("qa", [4, SEQ], BF16)
            bfc = sb("bfc", [4, 1], F32)
            for h in range(2):
                S.add("sp", lambda e, h=h: e.dma_start(out=fl[:, h * HALF:(h + 1) * HALF], in_=I["flT2"][h]), reads=IR.get("flT2", []), writes=[("fl", h)], dma=True, chan=("fl", h))
            S.add("sp", lambda e: e.dma_start(out=bfc[:], in_=W["bf"].rearrange("(p o) -> p o", o=1)), writes=["bfc"], dma=True, chan="bfc")
            S.add("dve", lambda e: e.tensor_scalar_mul(out=bfc[:], in0=bfc[:], scalar1=-1.0), reads=["bfc"], writes=["bfc"])
            S.add("pool", lambda e: e.memset(onesr[:], 1.0), writes=["onesr"])
            S.add("act", lambda e: e.activation(out=lg[:], in_=fl[:], func=AF.Exp, bias=bfc[:, 0:1], scale=-1.0),
                  reads=[("fl", 0), ("fl", 1), "bfc"], writes=["lg"])
            S.add("act", lambda e: e.activation(out=lg[:], in_=lg[:], func=AF.Ln, bias=1.0, scale=1.0), reads=["lg"], writes=["lg"])
            S.add("dve", lambda e: e.tensor_tensor_scan(out=ncum[:], data0=onesr[:], data1=lg[:], initial=0.0, op0=ALU.mult, op1=ALU.add),
                  reads=["lg", "onesr"], writes=["ncum"])
            S.add("dve", lambda e: e.tensor_scalar_mul(out=qa[:], in0=ncum[:], scalar1=-8.0), reads=["ncum"], writes=["qa"])
            S.add("sp", lambda e: e.dma_start(out=scr["caug"], in_=qa[:]), reads=["qa"], writes=["caug"], dma=True, chan="caug")
            for blk in range(NKB):
                S.add("pe", lambda e, blk=blk: e.transpose(out=psum[0][:, blk * 4:(blk + 1) * 4], in_=ncum[0:4, blk * 128:(blk + 1) * 128], identity=identf[0:4, 0:4]),
                      reads=["ncum", "identf"], writes=[("ps", 0)])
            S.add("dve", lambda e: e.tensor_copy(out=ncumT[:], in_=psum[0][:, 0:NKB * 4]), reads=[("ps", 0)], writes=["ncumT"])
        S.barrier()

        with ExitStack() as es:
            sb = lambda n, s, d: es.enter_context(nc.sbuf_tensor("p2b_" + n, s, d))
            wuq = sb("wuq", [128, 3, 384], BF16)
            wuqS = sb("wuqS", [128, 3, 384], BF16)
            wuk = sb("wuk", [128, 2, 4, 64], BF16)
            wuv = sb("wuv", [128, 2, 4, 64], BF16)
            gq = sb("gq", [128, 3], F32)
            gkv = sb("gkv", [128, 2], F32)
            onesm = sb("ones", [128, 128], BF16)
            cq = [sb(f"cq{i}", [128, 3, TT], F32) for i in range(2)]
            ckv = [sb(f"ckv{i}", [128, 2, TT], F32) for i in range(2)]
            krA = [sb(f"krA{i}", [96, TT], F32) for i in range(2)]
            krB = [sb(f"krB{i}", [96, TT], F32) for i in range(2)]
            cs = [sb(f"cs{i}", [96, 2, TT], F32) for i in range(2)]
            sq2 = [sb(f"sq{i}", [128, 3, TT], BF16) for i in range(2)]
            rstd2 = [sb(f"rstd{i}", [128, TT], F32) for i in range(2)]
            cqn2 = [sb(f"cqn{i}", [128, 3, TT], BF16) for i in range(2)]
            ckvn2 = [sb(f"ckvn{i}", [128, 2, TT], BF16) for i in range(2)]
            t1s = [sb(f"t1_{i}", [96, TT], F32) for i in range(2)]
            t2s = [sb(f"t2_{i}", [96, TT], F32) for i in range(2)]
            qst = [sb(f"qst{i}", [96, TT], BF16) for i in range(2)]
            kst = [sb(f"kst{i}", [64, TT], BF16) for i in range(2)]
            krs = [sb(f"krs{i}", [96, TT], BF16) for i in range(2)]
            vst = [sb(f"vst{i}", [128, 4, 64], BF16) for i in range(2)]
            S.add("pool", lambda e: e.memset(onesm[:], 1.0), writes=["ones"])
            S.add("pool", lambda e: e.memset(wuqS[:], 0.0), writes=["wuqS"])
            wq_v = W["wuq"].rearrange("(k p) c -> p k c", p=128)
            S.add("pool", lambda e: e.dma_start(out=wuq[:], in_=wq_v), writes=["wuq"], dma=True, chan="wuq")
            for h in range(4):
                S.add("pool", lambda e, h=h: e.dma_start(out=wuqS[:, :, h * 96 + 64:h * 96 + 80], in_=wq_v[:, :, h * 96 + 80:h * 96 + 96]),
                      reads=["wuqS"], writes=[("wuqS", h, 0)], dma=True, chan="wuqS")
                S.add("pool", lambda e, h=h: e.dma_start(out=wuqS[:, :, h * 96 + 80:h * 96 + 96], in_=wq_v[:, :, h * 96 + 64:h * 96 + 80]),
                      reads=["wuqS"], writes=[("wuqS", h, 1)], dma=True, chan="wuqS")
            wkv_v = W["wukv"].rearrange("(k p) (h two d) -> p k h two d", p=128, two=2, d=64)
            for k in range(2):
                S.add("pool", lambda e, k=k: e.dma_start(out=wuk[:, k], in_=wkv_v[:, k, :, 0, :]), writes=[("wuk", k)], dma=True, chan="wuk")
                S.add("pool", lambda e, k=k: e.dma_start(out=wuv[:, k], in_=wkv_v[:, k, :, 1, :]), writes=[("wuv", k)], dma=True, chan="wuv")
            S.add("sp", lambda e: e.dma_start(out=gq[:], in_=W["qn"].rearrange("(k p) -> p k", p=128), allow_slow_non_contiguous=True), writes=["gq"], dma=True, chan="gq")
            S.add("sp", lambda e: e.dma_start(out=gkv[:], in_=W["kvn"].rearrange("(k p) -> p k", p=128), allow_slow_non_contiguous=True), writes=["gkv"], dma=True, chan="gkv")
            wuqS_res = ["wuqS"] + [("wuqS", h, i) for h in range(4) for i in range(2)]
            pi = 0

            def nxt():
                nonlocal pi
                pi += 1
                return 1 + (pi % 7)

            for tc in range(SEQ // TT):
                b = tc % 2
                t0 = tc * TT
                sq, rstd, cqn, ckvn, t1, t2 = sq2[b], rstd2[b], cqn2[b], ckvn2[b], t1s[b], t2s[b]
                if "cq_chunks" in I:
                    hh_, tl = t0 // HALF, t0 % HALF
                    src_cq = I["cq_chunks"][:, hh_, :, tl:tl + TT].rearrange("k p n -> p k n")
                    src_ckv = I["ckv_chunks"][0:2, hh_, :, tl:tl + TT].rearrange("k p n -> p k n")
                    kr = lambda r0, r1, hh_=hh_, tl=tl: I["ckv_chunks"][2, hh_, r0 - 256:r1 - 256, tl:tl + TT]
                    rq, rkv = [I["cq_res"]], [I["ckv_res"]]
                else:
                    src_cq = hsl(I["cqT2"], 0, 384, t0, TT).rearrange("(k p) n -> p k n", p=128)
                    src_ckv = hsl(I["ckvT2"], 0, 256, t0, TT).rearrange("(k p) n -> p k n", p=128)
                    kr = lambda r0, r1, t0=t0: hsl(I["ckvT2"], r0, r1, t0, TT)
                    rq, rkv = [], []
                S.add("sp", lambda e, b=b, src_cq=src_cq: e.dma_start(out=cq[b][:], in_=src_cq), reads=rq, writes=[("cq", b)], dma=True, chan=("cq", b))
                S.add("sp", lambda e, b=b, src_ckv=src_ckv: e.dma_start(out=ckv[b][:], in_=src_ckv), reads=rkv, writes=[("ckv", b)], dma=True, chan=("ckv", b))
                S.add("sp", lambda e, b=b, a_=kr(256, 288): e.dma_start(out=krA[b][64:96, :], in_=a_), reads=rkv, writes=[("krA", b)], dma=True, chan=("krA", b))
                S.add("sp", lambda e, b=b, a_=kr(272, 288): e.dma_start(out=krB[b][64:80, :], in_=a_), reads=rkv, writes=[("krB", b, 0)], dma=True, chan=("krB", b))
                S.add("sp", lambda e, b=b, a_=kr(256, 272): e.dma_start(out=krB[b][80:96, :], in_=a_), reads=rkv, writes=[("krB", b, 1)], dma=True, chan=("krB", b))
                S.add("sp", lambda e, sq=sq, rstd=rstd, cqn=cqn, ckvn=ckvn, t1=t1, t2=t2, b=b, t0=t0: e.dma_start(out=cs[b][64:96, 0, :], in_=C["cosT"][:, t0:t0 + TT]), writes=[("cs", b, 0)], dma=True, chan=("cs", b))
                S.add("sp", lambda e, sq=sq, rstd=rstd, cqn=cqn, ckvn=ckvn, t1=t1, t2=t2, b=b, t0=t0: e.dma_start(out=cs[b][64:96, 1, :], in_=C["sinS"][:, t0:t0 + TT]), writes=[("cs", b, 1)], dma=True, chan=("cs", b))
                for (src, KC, dst, g, D, nm) in ((cq[b], 3, cqn, gq, 384, "cq"), (ckv[b], 2, ckvn, gkv, 256, "ckv")):
                    S.add("act", lambda e, sq=sq, rstd=rstd, cqn=cqn, ckvn=ckvn, t1=t1, t2=t2, src=src, KC=KC: e.activation(out=sq[:, 0:KC, :], in_=src[:], func=AF.Square), reads=[(nm, b)], writes=[("sq", b)])
                    pn = nxt()
                    for k in range(KC):
                        S.add("pe", lambda e, sq=sq, rstd=rstd, cqn=cqn, ckvn=ckvn, t1=t1, t2=t2, k=k, KC=KC, pn=pn: e.matmul(psum[pn][:], lhsT=onesm[:], rhs=sq[:, k, :], start=(k == 0), stop=(k == KC - 1)),
                              reads=[("sq", b), "ones"], writes=[("ps", pn)])
                    S.add("act", lambda e, sq=sq, rstd=rstd, cqn=cqn, ckvn=ckvn, t1=t1, t2=t2, pn=pn, D=D: e.activation(out=rstd[:], in_=psum[pn][:], func=AF.Sqrt, bias=EPS, scale=1.0 / D), reads=[("ps", pn)], writes=[("rstd", b)])
                    S.add("dve", lambda e, sq=sq, rstd=rstd, cqn=cqn, ckvn=ckvn, t1=t1, t2=t2: e.reciprocal(out=rstd[:], in_=rstd[:]), reads=[("rstd", b)], writes=[("rstd", b)])
                    for k in range(KC):
                        S.add("dve", lambda e, sq=sq, rstd=rstd, cqn=cqn, ckvn=ckvn, t1=t1, t2=t2, k=k, src=src, dst=dst, g=g: e.scalar_tensor_tensor(out=dst[:, k, :], in0=src[:, k, :], scalar=g[:, k:k + 1], in1=rstd[:],
                                                                                                op0=ALU.mult, op1=ALU.mult),
                              reads=[(nm, b), ("rstd", b), "gq", "gkv"], writes=[(nm + "n", b, k)])
                kb_ = tc % 2
                S.add("dve", lambda e, sq=sq, rstd=rstd, cqn=cqn, ckvn=ckvn, t1=t1, t2=t2, b=b: e.tensor_tensor(out=t1[64:96, :], in0=krA[b][64:96, :], in1=cs[b][64:96, 0, :], op=ALU.mult),
                      reads=[("krA", b), ("cs", b, 0)], writes=[("t1", b)])
                S.add("dve", lambda e, sq=sq, rstd=rstd, cqn=cqn, ckvn=ckvn, t1=t1, t2=t2, b=b: e.tensor_tensor(out=t2[64:96, :], in0=krB[b][64:96, :], in1=cs[b][64:96, 1, :], op=ALU.mult),
                      reads=[("krB", b, 0), ("krB", b, 1), ("cs", b, 1)], writes=[("t2", b)])
                S.add("dve", lambda e, sq=sq, rstd=rstd, cqn=cqn, ckvn=ckvn, t1=t1, t2=t2, kb_=kb_: e.tensor_tensor(out=krs[kb_][64:96, :], in0=t1[64:96, :], in1=t2[64:96, :], op=ALU.add),
                      reads=[("t1", b), ("t2", b)], writes=[("krs", kb_)])
                for h in range(4):
                    S.add("sp", lambda e, sq=sq, rstd=rstd, cqn=cqn, ckvn=ckvn, t1=t1, t2=t2, h=h, kb_=kb_, t0=t0: e.dma_start(out=scr["KT"][h, 64:96, t0:t0 + TT], in_=krs[kb_][64:96, :]),
                          reads=[("krs", kb_)], writes=[("KTr", h, tc)], dma=True, chan=("krs", kb_))
                for h in range(4):
                    qb = (tc * 4 + h) % 2
                    pa = nxt()
                    pb = nxt()
                    for k in range(3):
                        S.add("pe", lambda e, sq=sq, rstd=rstd, cqn=cqn, ckvn=ckvn, t1=t1, t2=t2, k=k, h=h, pa=pa: e.matmul(psum[pa][0:96, :], lhsT=wuq[:, k, h * 96:(h + 1) * 96], rhs=cqn[:, k, :], start=(k == 0), stop=(k == 2)),
                              reads=["wuq"] + [("cqn", b, kk) for kk in range(3)], writes=[("ps", pa)])
                    for k in range(3):
                        S.add("pe", lambda e, sq=sq, rstd=rstd, cqn=cqn, ckvn=ckvn, t1=t1, t2=t2, k=k, h=h, pb=pb: e.matmul(psum[pb][0:96, :], lhsT=wuqS[:, k, h * 96:(h + 1) * 96], rhs=cqn[:, k, :], start=(k == 0), stop=(k == 2)),
                              reads=wuqS_res + [("cqn", b, kk) for kk in range(3)], writes=[("ps", pb)])
                    S.add("act", lambda e, sq=sq, rstd=rstd, cqn=cqn, ckvn=ckvn, t1=t1, t2=t2, pa=pa, qb=qb: e.copy(out=qst[qb][0:64, :], in_=psum[pa][0:64, :]), reads=[("ps", pa)], writes=[("qst", qb, 0)])
                    S.add("dve", lambda e, sq=sq, rstd=rstd, cqn=cqn, ckvn=ckvn, t1=t1, t2=t2, pa=pa, b=b: e.tensor_tensor(out=t1[64:96, :], in0=psum[pa][64:96, :], in1=cs[b][64:96, 0, :], op=ALU.mult),
                          reads=[("ps", pa), ("cs", b, 0)], writes=[("t1", b)])
                    S.add("dve", lambda e, sq=sq, rstd=rstd, cqn=cqn, ckvn=ckvn, t1=t1, t2=t2, pb=pb, b=b: e.tensor_tensor(out=t2[64:96, :], in0=psum[pb][64:96, :], in1=cs[b][64:96, 1, :], op=ALU.mult),
                          reads=[("ps", pb), ("cs", b, 1)], writes=[("t2", b)])
                    S.add("dve", lambda e, sq=sq, rstd=rstd, cqn=cqn, ckvn=ckvn, t1=t1, t2=t2, qb=qb: e.tensor_tensor(out=qst[qb][64:96, :], in0=t1[64:96, :], in1=t2[64:96, :], op=ALU.add),
                          reads=[("t1", b), ("t2", b)], writes=[("qst", qb, 1)])
                    S.add("sp", lambda e, sq=sq, rstd=rstd, cqn=cqn, ckvn=ckvn, t1=t1, t2=t2, h=h, qb=qb, t0=t0: e.dma_start(out=scr["QT"][h, :, t0:t0 + TT], in_=qst[qb][:]),
                          reads=[("qst", qb, 0), ("qst", qb, 1)], writes=[("QT", h, tc)], dma=True, chan=("qst", qb))
                    pk = nxt()
                    for k in range(2):
                        S.add("pe", lambda e, sq=sq, rstd=rstd, cqn=cqn, ckvn=ckvn, t1=t1, t2=t2, k=k, h=h, pk=pk: e.matmul(psum[pk][0:64, :], lhsT=wuk[:, k, h, :], rhs=ckvn[:, k, :], start=(k == 0), stop=(k == 1)),
                              reads=[("wuk", 0), ("wuk", 1), ("ckvn", b, 0), ("ckvn", b, 1)], writes=[("ps", pk)])
                    S.add("act", lambda e, sq=sq, rstd=rstd, cqn=cqn, ckvn=ckvn, t1=t1, t2=t2, pk=pk, qb=qb: e.copy(out=kst[qb][:], in_=psum[pk][0:64, :]), reads=[("ps", pk)], writes=[("kst", qb)])
                    S.add("sp", lambda e, sq=sq, rstd=rstd, cqn=cqn, ckvn=ckvn, t1=t1, t2=t2, h=h, qb=qb, t0=t0: e.dma_start(out=scr["KT"][h, 0:64, t0:t0 + TT], in_=kst[qb][:]),
                          reads=[("kst", qb)], writes=[("KTn", h, tc)], dma=True, chan=("kst", qb))
                for tb in range(4):
                    vb = (tc * 4 + tb) % 2
                    pv = nxt()
                    for k in range(2):
                        S.add("pe", lambda e, sq=sq, rstd=rstd, cqn=cqn, ckvn=ckvn, t1=t1, t2=t2, k=k, tb=tb, pv=pv: e.matmul(psum[pv][:, 0:256], lhsT=ckvn[:, k, tb * 128:(tb + 1) * 128], rhs=wuv[:, k].rearrange("p h d -> p (h d)"),
                                                                         start=(k == 0), stop=(k == 1)),
                              reads=[("wuv", 0), ("wuv", 1), ("ckvn", b, 0), ("ckvn", b, 1)], writes=[("ps", pv)])
                    S.add("act", lambda e, sq=sq, rstd=rstd, cqn=cqn, ckvn=ckvn, t1=t1, t2=t2, pv=pv, vb=vb: e.copy(out=vst[vb][:].rearrange("p h d -> p (h d)"), in_=psum[pv][:, 0:256]), reads=[("ps", pv)], writes=[("vst", vb)])
                    tok = t0 + tb * 128
                    S.add("sp", lambda e, sq=sq, rstd=rstd, cqn=cqn, ckvn=ckvn, t1=t1, t2=t2, vb=vb, tok=tok: e.dma_start(out=scr["Vd"][:, tok:tok + 128, :].rearrange("h p d -> p h d"), in_=vst[vb][:]),
                          reads=[("vst", vb)], writes=[("Vd", tok // 128)], dma=True, chan=("vst", vb))
        S.barrier()

        if hook_c is not None:
            hook_c()
        with ExitStack() as es:
            sb = lambda n, s, d: es.enter_context(nc.sbuf_tensor("p2c_" + n, s, d))
            Qs = [sb(f"Q{i}", [96, SEQ], BF16) for i in range(2)]
            Ks = [sb(f"K{i}", [96, SEQ], BF16) for i in range(2)]
            Vs = [sb(f"V{i}", [128, NKB, 65], BF16) for i in range(2)]
            LA, NPT, SBANKS = 3, 6, (0, 1, 2, 6, 7)
            pT = [sb(f"pT{i}", [128, TT], BF16) for i in range(NPT)]
            ost = [sb(f"ost{i}", [64, TT], BF16) for i in range(2)]
            rrow = sb("rrow", [65, TT], F32)
            rbc = sb("rbc", [64, TT], F32)
            onesr2 = sb("onesr2", [65, 64], F32)
            S.add("pool", lambda e: e.memset(onesr2[:], 1.0), writes=["onesr2"])
            for i in range(2):
                S.add("pool", lambda e, i=i: e.memset(Vs[i][:, :, 64:65], 1.0), writes=[("Vone", i)])
            pti = 0
            poi = 0
            psi = 0
            pending = []
            import os
            for hi in range(int(os.environ.get('P2HEADS', '8'))):
                fox = hi < 4
                h = hi % 4
                sl = hi % 2
                R = 65 if fox else 96
                scale = 0.125 if fox else 96 ** -0.5
                ramp = rampF if fox else rampM
                rres = "rampF" if fox else "rampM"
                if fox:
                    for hf in range(2):
                        S.add("sp", lambda e, sl=sl, hf=hf, h=h: e.dma_start(out=Qs[sl][0:64, hf * HALF:(hf + 1) * HALF], in_=I["fqT2"][hf, h * 64:(h + 1) * 64, :]),
                              reads=IR.get("fqT2", []), writes=[("Q", sl, hf)], dma=True, chan=("Q", sl))
                        S.add("sp", lambda e, sl=sl, hf=hf, h=h: e.dma_start(out=Ks[sl][0:64, hf * HALF:(hf + 1) * HALF], in_=I["fkT2"][hf, h * 64:(h + 1) * 64, :]),
                              reads=IR.get("fkT2", []), writes=[("K", sl, hf)], dma=True, chan=("K", sl))
                        S.add("sp", lambda e, sl=sl, hf=hf, h=h: e.dma_start(out=Vs[sl][:, hf * 32:(hf + 1) * 32, 0:64],
                                                                             in_=I["fv2"][hf, :, h * 64:(h + 1) * 64].rearrange("(b p) d -> p b d", p=128)),
                              reads=IR.get("fv2", []), writes=[("V", sl, hf)], dma=True, chan=("V", sl))
                    S.add("sp", lambda e, sl=sl, h=h: e.dma_start(out=Qs[sl][64:65, :], in_=scr["caug"][h:h + 1, :]), reads=["caug"], writes=[("Q", sl, 2)], dma=True, chan=("Q", sl))
                    S.add("pool", lambda e, sl=sl: e.memset(Ks[sl][64:65, :], 1.0), writes=[("K", sl, 2)])
                else:
                    S.add("sp", lambda e, sl=sl, h=h: e.dma_start(out=Qs[sl][:], in_=scr["QT"][h]),
                          reads=[("QT", h, tc) for tc in range(SEQ // TT)], writes=[("Q", sl, 0), ("Q", sl, 1), ("Q", sl, 2)], dma=True, chan=("Q", sl))
                    S.add("sp", lambda e, sl=sl, h=h: e.dma_start(out=Ks[sl][:], in_=scr["KT"][h]),
                          reads=[("KTr", h, tc) for tc in range(SEQ // TT)] + [("KTn", h, tc) for tc in range(SEQ // TT)],
                          writes=[("K", sl, 0), ("K", sl, 1), ("K", sl, 2)], dma=True, chan=("K", sl))
                    S.add("sp", lambda e, sl=sl, h=h: e.dma_start(out=Vs[sl][:, :, 0:64], in_=scr["Vd"][h].rearrange("(b p) d -> p b d", p=128)),
                          reads=[("Vd", q) for q in range(NKB)], writes=[("V", sl, 0), ("V", sl, 1)], dma=True, chan=("V", sl))
                qres = [("Q", sl, i) for i in range(3)]
                kres = [("K", sl, i) for i in range(3)]
                vres = [("V", sl, 0), ("V", sl, 1), ("Vone", sl)]
                for i in range(SEQ // TT):
                    if hi == 4 and i == 1 and hook_mid is not None:
                        hook_mid()
                    nkb = 4 * i + 4
                    po = 3 + (poi % 2)
                    poi += 1
                    ptof = {}
                    for step in range(nkb + LA):
                        if step < nkb:
                            j = step
                            ps = SBANKS[psi % len(SBANKS)]
                            psi += 1
                            diag = j >= 4 * i
                            S.add("pe", lambda e, sl=sl, ps=ps, i=i, j=j, R=R, diag=diag: e.matmul(
                                psum[ps][:], lhsT=Ks[sl][0:R, j * 128:(j + 1) * 128], rhs=Qs[sl][0:R, i * TT:(i + 1) * TT], start=True, stop=not diag),
                                reads=qres + kres, writes=[("ps", ps)])
                            if diag:
                                off = (j - 4 * i) * 128
                                S.add("pe", lambda e, ps=ps, off=off, ramp=ramp: e.matmul(
                                    psum[ps][:], lhsT=identb[:], rhs=ramp[:, 384 - off:384 - off + TT], start=False, stop=True),
                                    reads=["identb", rres], writes=[("ps", ps)])
                            pt = pti % NPT
                            pti += 1
                            ptof[j] = pt
                            if fox:
                                S.add("act", lambda e, ps=ps, pt=pt, j=j, h=h, scale=scale: e.activation(
                                    out=pT[pt][:], in_=psum[ps][:], func=AF.Exp, bias=ncumT[:, j * 4 + h:j * 4 + h + 1], scale=scale),
                                    reads=[("ps", ps), "ncumT"], writes=[("pT", pt)])
                            else:
                                S.add("act", lambda e, ps=ps, pt=pt, scale=scale: e.activation(out=pT[pt][:], in_=psum[ps][:], func=AF.Exp, scale=scale),
                                      reads=[("ps", ps)], writes=[("pT", pt)])
                        if step == LA - 1 and pending:
                            pending.pop(0)()
                        if step >= LA:
                            j = step - LA
                            pt = ptof[j]
                            S.add("pe", lambda e, sl=sl, po=po, pt=pt, j=j, nkb=nkb: e.matmul(
                                psum[po][0:65, :], lhsT=Vs[sl][:, j, :], rhs=pT[pt][:], start=(j == 0), stop=(j == nkb - 1)),
                                reads=vres + [("pT", pt)], writes=[("ps", po)])
                    def epilogue(po=po, hi=hi, i=i, fox=fox, h=h):
                        S.add("dve", lambda e: e.reciprocal(out=rrow[64:65, :], in_=psum[po][64:65, :]), reads=[("ps", po)], writes=["rrow"])
                        S.add("pe", lambda e: e.matmul(psum[5][0:64, :], lhsT=onesr2[64:65, :], rhs=rrow[64:65, :], start=True, stop=True),
                              reads=["rrow", "onesr2"], writes=[("ps", 5)])
                        S.add("act", lambda e: e.copy(out=rbc[:], in_=psum[5][0:64, :]), reads=[("ps", 5)], writes=["rbc"])
                        ob = (hi * 16 + i) % 2
                        S.add("dve", lambda e: e.tensor_tensor(out=ost[ob][:], in0=psum[po][0:64, :], in1=rbc[:], op=ALU.mult),
                              reads=[("ps", po), "rbc"], writes=[("ost", ob)])
                        br = 2 if fox else 1
                        S.add("sp", lambda e: e.dma_start(out=brT_out[br, h * 64:(h + 1) * 64, i * TT:(i + 1) * TT], in_=ost[ob][:]),
                              reads=[("ost", ob)], writes=[("brout", br, h, i)], dma=True, chan=("ost", ob))
                    pending.append(epilogue)
            while pending:
                pending.pop(0)()
        S.barrier()

        if hook_d is not None:
            hook_d()
        with ExitStack() as es:
            sb = lambda n, s, d: es.enter_context(nc.sbuf_tensor("p2d_" + n, s, d))
            u = sb("u", [128, TL + 3], F32)
            ug = sb("ug", [128, TL], F32)
            xc = sb("xc", [128, TL], F32)
            xcb = sb("xcb", [128, TL], BF16)
            rr = sb("rr", [128, TL], F32)
            ig = sb("ig", [128, TL], F32)
            aa = sb("aa", [128, TL], F32)
            s1 = sb("s1", [128, TL], F32)
            bb = sb("bb", [128, TL], F32)
            hh = sb("hh", [128, TL], F32)
            g1 = sb("g1", [128, TL], F32)
            g2 = sb("g2", [128, TL], F32)
            osb = [sb(f"osb{i}", [128, TL], BF16) for i in range(2)]
            wab = sb("wab", [128, 2, 128], BF16)
            cv = sb("cv", [128, 12], F32)
            for ct in range(2):
                c0 = ct * 128
                S.add("pool", lambda e: e.memset(wab[:], 0.0), writes=["wab"])
                for q in range(2):
                    hd = ct * 2 + q
                    S.add("pool", lambda e, q=q, hd=hd: e.dma_start(out=wab[q * 64:(q + 1) * 64, 0, q * 64:(q + 1) * 64], in_=W["wa"][hd]),
                          reads=["wab"], writes=[("wab", 0, q)], dma=True, chan="wab")
                    S.add("pool", lambda e, q=q, hd=hd: e.dma_start(out=wab[q * 64:(q + 1) * 64, 1, q * 64:(q + 1) * 64], in_=W["wx"][hd]),
                          reads=["wab"], writes=[("wab", 1, q)], dma=True, chan="wab")
                wabres = ["wab"] + [("wab", i, q) for i in range(2) for q in range(2)]
                S.add("sp", lambda e, c0=c0: e.dma_start(out=cv[:, 0:4], in_=W["conv_w"][:, c0:c0 + 128].rearrange("k p -> p k"), allow_slow_non_contiguous=True),
                      writes=[("cv", 0)], dma=True, chan="cv")
                for ci, nm in ((4, "conv_b"), (5, "ba"), (6, "bx"), (7, "lam")):
                    S.add("sp", lambda e, ci=ci, nm=nm, c0=c0: e.dma_start(out=cv[:, ci:ci + 1], in_=W[nm][c0:c0 + 128].rearrange("(p o) -> p o", o=1)),
                          writes=[("cv", ci)], dma=True, chan="cv")
                S.add("act", lambda e: e.activation(out=cv[:, 8:9], in_=cv[:, 7:8], func=AF.Exp, scale=-1.0), reads=[("cv", 7)], writes=[("cv", 8)])
                S.add("act", lambda e: e.activation(out=cv[:, 8:9], in_=cv[:, 8:9], func=AF.Ln, bias=1.0, scale=1.0), reads=[("cv", 8)], writes=[("cv", 8)])
                S.add("dve", lambda e: e.tensor_scalar_mul(out=cv[:, 9:10], in0=cv[:, 8:9], scalar1=-16.0), reads=[("cv", 8)], writes=[("cv", 9)])
                S.add("dve", lambda e: e.tensor_scalar_mul(out=cv[:, 8:9], in0=cv[:, 8:9], scalar1=-8.0), reads=[("cv", 8), ("cv", 9)], writes=[("cv", 8)])
                cvall = [("cv", i) for i in range(10)]
                for c in range(SEQ // TL):
                    t0 = c * TL
                    S.add("sp", lambda e, c0=c0, t0=t0: e.dma_start(out=u[:, 3:], in_=hsl(I["uT2"], c0, c0 + 128, t0, TL)), reads=IR.get("uT2", []), writes=[("u", 1)], dma=True, chan="u")
                    if c == 0:
                        S.add("pool", lambda e: e.memset(u[:, 0:3], 0.0), writes=[("u", 0)])
                    else:
                        S.add("sp", lambda e, c0=c0, t0=t0: e.dma_start(out=u[:, 0:3], in_=hsl(I["uT2"], c0, c0 + 128, t0 - 3, 3)), reads=IR.get("uT2", []), writes=[("u", 0)], dma=True, chan="u0")
                    S.add("sp", lambda e, c0=c0, t0=t0: e.dma_start(out=ug[:], in_=hsl(I["uT2"], 256 + c0, 256 + c0 + 128, t0, TL)), reads=IR.get("uT2", []), writes=["ug"], dma=True, chan="ug")
                    ures = [("u", 0), ("u", 1)]
                    S.add("dve", lambda e: e.tensor_scalar(out=xc[:], in0=u[:, 0:TL], scalar1=cv[:, 0:1], scalar2=cv[:, 4:5], op0=ALU.mult, op1=ALU.add),
                          reads=ures + cvall, writes=["xc"])
                    for k in range(1, 4):
                        S.add("dve", lambda e, k=k: e.scalar_tensor_tensor(out=xc[:], in0=u[:, k:k + TL], scalar=cv[:, k:k + 1], in1=xc[:], op0=ALU.mult, op1=ALU.add),
                              reads=ures + cvall + ["xc"], writes=["xc"])
                    S.add("pool", lambda e: e.tensor_copy(out=xcb[:], in_=xc[:]), reads=["xc"], writes=["xcb"])
                    for sbi in range(TL // TT):
                        ssl = slice(sbi * TT, (sbi + 1) * TT)
                        pr = 6 + (sbi % 2)
                        S.add("pe", lambda e, pr=pr, ssl=ssl: e.matmul(psum[pr][:], lhsT=wab[:, 0, :], rhs=xcb[:, ssl], start=True, stop=True),
                              reads=wabres + ["xcb"], writes=[("ps", pr)])
                        S.add("act", lambda e, pr=pr, ssl=ssl: e.activation(out=rr[:, ssl], in_=psum[pr][:], func=AF.Sigmoid, bias=cv[:, 5:6], scale=1.0),
                              reads=[("ps", pr)] + cvall, writes=[("rr", sbi)])
                        pr2 = 1 + (sbi % 2)
                        S.add("pe", lambda e, pr2=pr2, ssl=ssl: e.matmul(psum[pr2][:], lhsT=wab[:, 1, :], rhs=xcb[:, ssl], start=True, stop=True),
                              reads=wabres + ["xcb"], writes=[("ps", pr2)])
                        S.add("act", lambda e, pr2=pr2, ssl=ssl: e.activation(out=ig[:, ssl], in_=psum[pr2][:], func=AF.Sigmoid, bias=cv[:, 6:7], scale=1.0),
                              reads=[("ps", pr2)] + cvall, writes=[("ig", sbi)])
                    rrres = [("rr", i) for i in range(TL // TT)]
                    igres = [("ig", i) for i in range(TL // TT)]
                    S.add("act", lambda e: e.activation(out=aa[:], in_=rr[:], func=AF.Exp, scale=cv[:, 8:9]), reads=rrres + cvall, writes=["aa"])
                    S.add("act", lambda e: e.activation(out=s1[:], in_=rr[:], func=AF.Exp, scale=cv[:, 9:10]), reads=rrres + cvall, writes=["s1"])
                    S.add("dve", lambda e: e.tensor_scalar_min(out=s1[:], in0=s1[:], scalar1=1.0), reads=["s1"], writes=["s1"])
                    S.add("act", lambda e: e.activation(out=s1[:], in_=s1[:], func=AF.Sqrt, bias=1.0, scale=-1.0), reads=["s1"], writes=["s1"])
                    S.add("dve", lambda e: e.tensor_tensor(out=bb[:], in0=ig[:], in1=xc[:], op=ALU.mult), reads=igres + ["xc"], writes=["bb"])
                    S.add("dve", lambda e: e.tensor_tensor(out=bb[:], in0=bb[:], in1=s1[:], op=ALU.mult), reads=["bb", "s1"], writes=["bb"])
                    if c == 0:
                        S.add("dve", lambda e: e.tensor_tensor_scan(out=hh[:], data0=aa[:], data1=bb[:], initial=0.0, op0=ALU.mult, op1=ALU.add),
                              reads=["aa", "bb"], writes=["hh"])
                    else:
                        S.add("dve", lambda e: e.tensor_tensor_scan(out=hh[:], data0=aa[:], data1=bb[:], initial=cv[:, 10:11], op0=ALU.mult, op1=ALU.add),
                              reads=["aa", "bb", "carry"], writes=["hh"])
                    S.add("dve", lambda e: e.tensor_copy(out=cv[:, 10:11], in_=hh[:, TL - 1:TL]), reads=["hh"], writes=["carry"])
                    S.add("pool", lambda e: e.tensor_tensor(out=g1[:], in0=ug[:], in1=ug[:], op=ALU.mult), reads=["ug"], writes=["g1"])
                    S.add("pool", lambda e: e.tensor_scalar(out=g1[:], in0=g1[:], scalar1=0.044715, scalar2=1.0, op0=ALU.mult, op1=ALU.add), reads=["g1"], writes=["g1"])
                    S.add("pool", lambda e: e.tensor_tensor(out=g1[:], in0=g1[:], in1=ug[:], op=ALU.mult), reads=["g1", "ug"], writes=["g1"])
                    S.add("act", lambda e: e.activation(out=g2[:], in_=g1[:], func=AF.Sigmoid, scale=1.5957691216057308), reads=["g1"], writes=["g2"])
                    S.add("pool", lambda e: e.tensor_tensor(out=g2[:], in0=g2[:], in1=ug[:], op=ALU.mult), reads=["g2", "ug"], writes=["g2"])
                    ob = (ct * 4 + c) % 2
                    S.add("dve", lambda e, ob=ob: e.tensor_tensor(out=osb[ob][:], in0=hh[:], in1=g2[:], op=ALU.mult), reads=["hh", "g2"], writes=[("osb", ob)])
                    S.add("sp", lambda e, ob=ob, c0=c0, t0=t0: e.dma_start(out=brT_out[0, c0:c0 + 128, t0:t0 + TL], in_=osb[ob][:]),
                          reads=[("osb", ob)], writes=[("brout", 0, ct, c)], dma=True, chan=("osb", ob))
    S.barrier()


PAIRS = [[0, 1], [2, 3], [4, 5], [6, 7]]


def _gather_chunks(S, nc, name, src, rc, rd, chan="cc"):
    R, N = src.shape
    nch = R // rc
    assert nch * rc == R
    G = nc.dram_tensor(name, [nch, 2 * rc, N], src.dtype).ap()
    for i in range(nch):
        S.add("pool", lambda e, i=i: e.collective_compute("AllGather", ALU.bypass, replica_groups=PAIRS, ins=[src[i * rc:(i + 1) * rc, :]], outs=[G[i]]),
              reads=rd, writes=[("G", name)], dma=True, chan=chan, inc=1)
    return G.rearrange("c (h r) n -> c h r n", h=2)


def exchange1(S, nc, T, p1o, gsel, dynq="sp"):
    dt_ = lambda n, s, d=F32: nc.dram_tensor(T + n, s, d).ap()
    I = dict(uT2=dt_("i_uT2", [2, 512, NTOK]), fqT2=dt_("i_fqT2", [2, 256, NTOK], BF16), fkT2=dt_("i_fkT2", [2, 256, NTOK], BF16),
             fv2=dt_("i_fv2", [2, NTOK, 256], BF16), flT2=dt_("i_flT2", [2, 4, NTOK]))
    Gl = _gather_chunks(S, nc, T + "g_flT", p1o["flT"], 8, [("s1", "flT")], chan="cc0").rearrange("c h (q r) n -> c h q r n", q=2)
    S.add(dynq, lambda e: e.dma_start(out=I["flT2"].rearrange("h (o r) n -> h o r n", o=1), in_=Gl[0, :, bass.ds(gsel(e)[0], 1)]),
          reads=[("G", T + "g_flT")], writes=[("I", "flT2")], dma=True, chan=("rg", dynq, 0))
    I["cq_chunks"] = _gather_chunks(S, nc, T + "g_cqT", p1o["cqT"], 128, [("s1", "cqT")], chan="cc0")
    I["ckv_chunks"] = _gather_chunks(S, nc, T + "g_ckvT", p1o["ckvT"], 128, [("s1", "ckvT")], chan="cc0")
    I["cq_res"] = ("G", T + "g_cqT")
    I["ckv_res"] = ("G", T + "g_ckvT")
    Gq = _gather_chunks(S, nc, T + "g_fqT", p1o["fqT"], 256, [("s1", "fqT")], chan="cc1")
    Gk = _gather_chunks(S, nc, T + "g_fkT", p1o["fkT"], 256, [("s1", "fkT")], chan="cc1")
    Gv = _gather_chunks(S, nc, T + "g_fv", p1o["fv"], 2048, [("s1", "fv")], chan="cc1").rearrange("c h t (q d) -> c h t q d", q=2)
    Gu = _gather_chunks(S, nc, T + "g_uT", p1o["uT"], 128, [("s1", "uT")], chan="cc1").rearrange("(q c) h r n -> q c h r n", c=2)

    def late():
        fns = []
        fns.append((lambda e: e.dma_start(out=I["fqT2"].unsqueeze(0), in_=Gq[bass.ds(gsel(e)[0], 1)]), "g_fqT", "fqT2"))
        fns.append((lambda e: e.dma_start(out=I["fkT2"].unsqueeze(0), in_=Gk[bass.ds(gsel(e)[0], 1)]), "g_fkT", "fkT2"))
        for hf in range(2):
            for c in range(2):
                fns.append((lambda e, hf=hf, c=c: e.dma_start(out=I["fv2"][hf, c * 2048:(c + 1) * 2048, :].rearrange("t (o d) -> t o d", o=1),
                                                              in_=Gv[c, hf, :, bass.ds(gsel(e)[0], 1), :]), "g_fv", "fv2"))
        for hf in range(2):
            fns.append((lambda e, hf=hf: e.dma_start(out=I["uT2"][hf:hf + 1, 0:256, :].rearrange("o (c r) n -> o c r n", c=2), in_=Gu[0:2][bass.ds(gsel(e)[0], 1), :, hf]), "g_uT", "uT2"))
            fns.append((lambda e, hf=hf: e.dma_start(out=I["uT2"][hf:hf + 1, 256:512, :].rearrange("o (c r) n -> o c r n", c=2), in_=Gu[2:4][bass.ds(gsel(e)[0], 1), :, hf]), "g_uT", "uT2"))
        for i, (fn, src, dst) in enumerate(fns):
            S.add(dynq, fn, reads=[("G", T + src)], writes=[("I", dst, i)], dma=True, chan=("rg", dynq, 1))
    I["res"] = dict(flT2=[("I", "flT2")], fqT2=[("I", "fqT2", 0)], fkT2=[("I", "fkT2", 1)], fv2=[("I", "fv2", i) for i in range(2, 6)],
                    uT2=[("I", "uT2", i) for i in range(6, 10)])
    return I, late


class Exchange2:
    def __init__(self, S, nc, T, br_mine):
        self.S, self.nc, self.T = S, nc, T
        self.src = br_mine.rearrange("j r n -> (j r) n")
        self.G = nc.dram_tensor(T + "g_br", [6, 256, 8192], BF16).ap()

    def gather(self, j, reads):
        for rh in range(2):
            c = j * 2 + rh
            self.S.add("pool", lambda e, c=c: e.collective_compute("AllGather", ALU.bypass, replica_groups=PAIRS,
                                                                     ins=[self.src[c * 128:(c + 1) * 128, :]], outs=[self.G[c]]),
                       reads=reads, writes=[("Gbr", j, rh)], dma=True, chan="cc2", inc=1)

    def regroup(self, gsel, dynq):
        S, T = self.S, self.T
        br_in = self.nc.dram_tensor(T + "br_in", [3, 512, NTOK], BF16).ap()
        Gb = self.G.rearrange("(j rh) (g r) (h n) -> j rh g r h n", j=3, g=2, h=2)
        for j in range(3):
            for gp in range(2):
                S.add(dynq, lambda e, j=j, gp=gp: e.dma_start(out=br_in[j, gp * 256:(gp + 1) * 256, :].rearrange("(rh r) (o n) -> rh r o n", rh=2, o=1),
                                                             in_=Gb[j, :, gp, :, bass.ds(gsel(e)[0], 1), :]),
                      reads=[("Gbr", j, 0), ("Gbr", j, 1)], writes=[("brin", j, gp)], dma=True, chan=("rg", dynq))
        return br_in


def make_consts():
    ident = np.eye(128, dtype=np.float32)
    p = np.arange(128)[:, None]; g = np.arange(896)[None, :]
    ramp_fox = np.where(g - p >= 384, 0.0, -30000.0).astype(np.float32)
    ramp_mla = np.where((g // 64) >= (p // 64) + 6, 0.0, -30000.0).astype(np.float32)
    pos = np.arange(8192, dtype=np.float32)
    inv_freq = (np.float32(10000.0) ** (-np.arange(0, 32, 2, dtype=np.float32) / np.float32(32))).astype(np.float32)
    ang = (pos[:, None] * inv_freq[None, :]).astype(np.float32)
    cos = np.cos(ang).astype(np.float32).T; sin = np.sin(ang).astype(np.float32).T
    cosT = np.ascontiguousarray(np.concatenate([cos, cos], 0))
    sinS = np.ascontiguousarray(np.concatenate([-sin, sin], 0))
    return dict(ident=ident, ramp_fox=ramp_fox, ramp_mla=ramp_mla, cosT=cosT, sinS=sinS)


_BF = ml_dtypes.bfloat16
_PROGS = {}
DEPTH = 2
XQ = ["sp", "pool"]


class NcTag:
    def __init__(self, nc, tag):
        self._nc = nc
        self._tag = tag

    def sbuf_tensor(self, name, shape, dt):
        return self._nc.sbuf_tensor(self._tag + name, shape, dt)

    def psum_tensor(self, name, shape, dt):
        return self._nc.psum_tensor(self._tag + name, shape, dt)

    def __getattr__(self, a):
        return getattr(self._nc, a)


def _di(nc, n, s, d=F32):
    return nc.dram_tensor(n, s, d, kind="ExternalInput").ap()


def build_fused(stage=9):
    nc = bass.Bass("TRN2", target_bir_lowering=False)
    xT = _di(nc, "xT", [1024, NTOK])
    pT = _di(nc, "pT", [DEPTH, 256, NTOK])
    Wf = dict(w_in=_di(nc, "w_in", [DEPTH, 1024, D_IN]), gate_b=_di(nc, "gate_b", [DEPTH, 3072]), mixg=_di(nc, "mixg", [DEPTH, 1024]),
              ffng=_di(nc, "ffng", [DEPTH, 1024]), pleg=_di(nc, "pleg", [DEPTH, 1024]), w_o=_di(nc, "w_o", [DEPTH, 1024, 1024]),
              w_gu=_di(nc, "w_gu", [DEPTH, 1024, 5632]), w_down=_di(nc, "w_down", [DEPTH, 2816, 1024]), w_pg=_di(nc, "w_pg", [DEPTH, 1024, 1024]),
              w_ple=_di(nc, "w_ple", [DEPTH, 256, 1024]), w_br0=_di(nc, "w_br0", [DEPTH, 512, 1024]), w_br1=_di(nc, "w_br1", [DEPTH, 512, 1024]),
              w_br2=_di(nc, "w_br2", [DEPTH, 512, 1024]), fing=_di(nc, "fing", [1024]))
    Wg = dict(conv_w=_di(nc, "conv_w", [DEPTH, 4, 256]), conv_b=_di(nc, "conv_b", [DEPTH, 256]), wa=_di(nc, "wa", [DEPTH, 4, 64, 64]), ba=_di(nc, "ba", [DEPTH, 256]),
              wx=_di(nc, "wx", [DEPTH, 4, 64, 64]), bx=_di(nc, "bx", [DEPTH, 256]), lam=_di(nc, "lam", [DEPTH, 256]), qn=_di(nc, "qn", [DEPTH, 384]),
              wuq=_di(nc, "wuq", [DEPTH, 384, 384]), kvn=_di(nc, "kvn", [DEPTH, 256]), wukv=_di(nc, "wukv", [DEPTH, 256, 512]), bf=_di(nc, "bf", [DEPTH, 4]))
    C = dict(ident=_di(nc, "ident", [128, 128]), ramp_fox=_di(nc, "ramp_fox", [128, 896]), ramp_mla=_di(nc, "ramp_mla", [128, 896]),
             cosT=_di(nc, "cosT", [32, 8192]), sinS=_di(nc, "sinS", [32, 8192]))
    out = nc.dram_tensor("out", [1024, NTOK], F32, kind="ExternalOutput").ap()
    dt_ = lambda n, s, d=F32: nc.dram_tensor(n, s, d).ap()
    S = Sched(nc)
    xres = xT
    _pid = {}

    def gsel(e):
        if id(e) not in _pid:
            _pid[id(e)] = (e.partition_id() % 2,)
        return _pid[id(e)]
    for l in range(DEPTH):
        T = f"L{l}_"
        nct = NcTag(nc, T)
        p1o = {nm: dt_(T + "s1_" + nm, shp, d) for nm, shp, d in
               [("uT", [1024, NTOK], F32), ("cqT", [384, NTOK], F32), ("ckvT", [384, NTOK], F32), ("fqT", [512, NTOK], BF16),
                ("fkT", [512, NTOK], BF16), ("fv", [NTOK, 512], BF16), ("flT", [8, NTOK], F32)]}
        phase1(S, nct, xres, Wf["w_in"][l], Wf["mixg"][l], p1o)
        I, late1 = exchange1(S, nc, T, p1o, gsel, dynq=XQ[l])
        if stage == 1:
            S.add("sp", lambda e, I=I: e.dma_start(out=out[0:4, :], in_=I["flT2"][0]), writes=["dummy"], dma=True, chan="dummy")
            break
        Wl = {k: v[l] for k, v in Wg.items()}
        br_mine = dt_(T + "br_mine", [3, 256, 8192], BF16)
        scr = dict(QT=dt_(T + "sQT", [4, 96, 8192], BF16), KT=dt_(T + "sKT", [4, 96, 8192], BF16),
                   Vd=dt_(T + "sVd", [4, 8192, 64], BF16), caug=dt_(T + "scaug", [4, 8192], BF16))
        wb = {}

        def precast(l=l, T=T, wb=wb):
            srcs = dict(w_o=Wf["w_o"][l], w_gu=Wf["w_gu"][l], w_down=Wf["w_down"][l], w_pg=Wf["w_pg"][l], w_ple=Wf["w_ple"][l],
                        w_br0=Wf["w_br0"][l], w_br1=Wf["w_br1"][l], w_br2=Wf["w_br2"][l], w_gate=Wf["w_in"][l][:, C_GATE:C_GATE + 3072])
            for nm, src in srcs.items():
                R, Cc = src.shape
                dst = dt_(T + "bf_" + nm, [R, Cc], BF16)
                wb[nm] = dst
                for r0 in range(0, R, 256):
                    r1 = min(R, r0 + 256)
                    S.add("pool", lambda e, src=src, dst=dst, r0=r0, r1=r1: e.dma_start(out=dst[r0:r1, :], in_=src[r0:r1, :]),
                          writes=[("wbf", nm, r0)], dma=True, chan="precast")

        ex2 = Exchange2(S, nc, T, br_mine)
        rows = lambda br: [("brout", br, h, i) for h in range(4) for i in range(16)]
        phase2(S, nct, I, Wl, C, br_mine, scr, hook_c=lambda: (late1(), precast()),
               hook_mid=lambda: ex2.gather(2, rows(2)), hook_d=lambda: ex2.gather(1, rows(1)))
        if stage == 2:
            S.add("sp", lambda e, I=I: e.dma_start(out=out[0:4, :], in_=I["flT2"][0]), writes=["dummy"], dma=True, chan="dummy")
            break
        ex2.gather(0, [("brout", 0, ct, c) for ct in range(2) for c in range(4)])
        br_in = ex2.regroup(gsel, XQ[l])
        S.barrier()
        if stage == 3:
            S.add("sp", lambda e, I=I: e.dma_start(out=out[0:4, :], in_=I["flT2"][0]), writes=["dummy"], dma=True, chan="dummy")
            break
        final = (l == DEPTH - 1) or stage in (4, 5)
        W3 = dict(w_in=Wf["w_in"][l], gate_b=Wf["gate_b"][l], mixg=Wf["mixg"][l], ffng=Wf["ffng"][l], pleg=Wf["pleg"][l], w_o=wb["w_o"],
                  w_gu=wb["w_gu"], w_down=wb["w_down"], w_pg=wb["w_pg"], w_ple=wb["w_ple"], w_gate=wb["w_gate"],
                  w_br=[wb["w_br0"], wb["w_br1"], wb["w_br2"]])
        if final:
            W3["fing"] = Wf["fing"]
            phase3(S, nct, xres, out, br_in, pT[l], W3, final_out=out, wq="sp")
            if stage in (4, 5):
                break
        else:
            xnext = dt_(T + "xnext", [1024, NTOK])
            phase3(S, nct, xres, xnext, br_in, pT[l], W3, wq="sp")
            xres = xnext
    S.emit()
    return nc


def kernel(x, p, mix_norm, w_in, gate_b, conv_w, conv_b, lru_wa, lru_ba, lru_wx, lru_bx, lru_lambda,
           mla_q_norm, mla_wuq, mla_kv_norm, mla_wukv, fox_bf, w_br_a, w_br_b, w_br_c, w_o,
           ffn_norm, w_gate_up, w_down, ple_norm, w_ple_gate, w_ple, final_norm):
    A = lambda a: np.ascontiguousarray(np.asarray(a))
    x = np.asarray(x)
    p = np.asarray(p)
    cores = list(range(8))
    Cn = make_consts()
    if "fused" not in _PROGS:
        import os
        _PROGS["fused"] = build_fused(int(os.environ.get("FSTAGE", "9")))
    nc = _PROGS["fused"]
    full = dict(w_in=A(w_in), gate_b=A(gate_b), mixg=A(mix_norm), ffng=A(ffn_norm), pleg=A(ple_norm), w_o=A(w_o), w_gu=A(w_gate_up),
                w_down=A(w_down), w_pg=A(w_ple_gate), w_ple=A(w_ple), w_br0=A(w_br_a), w_br1=A(w_br_b), w_br2=A(w_br_c), fing=A(final_norm),
                qn=A(mla_q_norm), kvn=A(mla_kv_norm))
    full.update(Cn)
    in_maps = []
    for c in cores:
        b, g = c // 2, c % 2
        m = dict(full)
        m["xT"] = A(x[b, g * NTOK:(g + 1) * NTOK].T)
        m["pT"] = A(np.transpose(p[:, b, g * NTOK:(g + 1) * NTOK], (0, 2, 1)))
        m.update(conv_w=A(np.asarray(conv_w)[:, :, g * 256:(g + 1) * 256]), conv_b=A(np.asarray(conv_b)[:, g * 256:(g + 1) * 256]),
                 wa=A(np.asarray(lru_wa)[:, g * 4:(g + 1) * 4]), ba=A(np.asarray(lru_ba)[:, g * 256:(g + 1) * 256]),
                 wx=A(np.asarray(lru_wx)[:, g * 4:(g + 1) * 4]), bx=A(np.asarray(lru_bx)[:, g * 256:(g + 1) * 256]),
                 lam=A(np.asarray(lru_lambda)[:, g * 256:(g + 1) * 256]), wuq=A(np.asarray(mla_wuq)[:, :, g * 384:(g + 1) * 384]),
                 wukv=A(np.asarray(mla_wukv)[:, :, g * 512:(g + 1) * 512]), bf=A(np.asarray(fox_bf)[:, g * 4:(g + 1) * 4]))
        in_maps.append(m)
    res = run_bass_kernel_spmd(nc, in_maps, core_ids=cores).results
    out = np.empty((4, 8192, 1024), np.float32)
    for c in cores:
        out[c // 2, (c % 2) * NTOK:(c % 2 + 1) * NTOK] = np.asarray(res[c]["out"]).T
    return out
```

```python
from contextlib import ExitStack
import numpy as np
import ml_dtypes
import concourse.bass as bass
import concourse.mybir as mybir
from concourse.bass_utils import run_bass_kernel_spmd

F32 = mybir.dt.float32
BF16 = mybir.dt.bfloat16
AF = mybir.ActivationFunctionType
ALU = mybir.AluOpType
AX = mybir.AxisListType

ENGS = ["pe", "act", "dve", "pool", "sp"]
EMBED_WAIT = True
BLOCKNAME = {"pe": "tensor", "act": "scalar", "dve": "vector", "pool": "gpsimd", "sp": "sync"}


class Sched:
    def __init__(self, nc, same_engine_sync=True):
        self.nc = nc
        self.ops = []
        self.lastw = {}
        self.readers = {}
        self.chan_count = {}
        self.chan_inc = {}
        self.barrier_deps = set()
        self.last_on_eng = {}
        self.last_on_chan = {}
        self.same_engine_sync = same_engine_sync
        self.seg = 0

    def add(self, eng, fn, reads=(), writes=(), dma=False, chan=None, inc=16):
        oid = len(self.ops)
        deps = set(self.barrier_deps)
        for r in reads:
            w = self.lastw.get(r)
            if w is not None:
                deps.add(w)
        for w_ in writes:
            w = self.lastw.get(w_)
            if w is not None:
                deps.add(w)
            deps.update(self.readers.get(w_, ()))
        deps = {(self.last_on_chan[self.ops[d]["chan"]] if self.ops[d]["dma"] else d) for d in deps}
        for r in reads:
            self.readers.setdefault(r, []).append(oid)
        for w_ in writes:
            self.lastw[w_] = oid
            self.readers[w_] = []
        op = dict(eng=eng, fn=fn, deps=deps, dma=dma, chan=chan, needed=False, sig=None, seg=self.seg)
        if dma:
            assert chan is not None
            c = self.chan_count.get(chan, 0) + 1
            self.chan_count[chan] = c
            op["sig"] = (("dma", chan), inc * c)
            op["inc"] = inc
            self.chan_inc[chan] = inc
            self.last_on_chan[chan] = oid
        else:
            self.last_on_eng[eng] = oid
        self.ops.append(op)
        return oid

    def barrier(self):
        self.seg += 1
        self.barrier_deps = set(self.last_on_eng.values()) | set(self.last_on_chan.values())

    def finalize(self):
        ops = self.ops
        for op in ops:
            pruned = {}
            for d in op["deps"]:
                dop = ops[d]
                key = ("dma", dop["chan"]) if dop["dma"] else ("eng", dop["eng"])
                if key not in pruned or d > pruned[key]:
                    pruned[key] = d
            pd = []
            for key, d in pruned.items():
                if key[0] == "eng" and key[1] == op["eng"] and not op["dma"]:
                    if op["eng"] == "pe" or not self.same_engine_sync:
                        continue
                pd.append(d)
            op["pdeps"] = pd
            op["deps"] = None
            for d in pd:
                ops[d]["needed"] = True
        cnt = {e: 0 for e in ENGS}
        for op in ops:
            if not op["dma"] and op["needed"]:
                cnt[op["eng"]] += 1
                op["sig"] = (("eng", op["eng"]), cnt[op["eng"]])
        self.final_counts = cnt

    def emit(self):
        nc = self.nc
        self.finalize()
        ops = self.ops
        with ExitStack() as st:
            sems = {}
            for e in ENGS:
                sems[("eng", e)] = st.enter_context(nc.semaphore(f"s_{e}"))
            for i, ch in enumerate(self.chan_count):
                sems[("dma", ch)] = st.enter_context(nc.semaphore(f"d{i}"))
            seen_all = {e: {} for e in ENGS}
            nseg = self.seg + 1
            for sg in range(nseg):
                segops = [op for op in ops if op["seg"] == sg]
                if not segops and sg != nseg - 1:
                    continue
                with nc.Block() as block:
                    for e in ENGS:
                        oplist = [op for op in segops if op["eng"] == e]
                        last = (sg == nseg - 1)

                        def body(eh, oplist=oplist, e=e, last=last):
                            seen = seen_all[e]
                            for op in oplist:
                                need = []
                                for d in op["pdeps"]:
                                    key, val = ops[d]["sig"]
                                    if seen.get(key, 0) < val:
                                        need.append((key, val))
                                        seen[key] = val
                                emb = need.pop() if (need and EMBED_WAIT and not op["dma"]) else None
                                for key, val in need:
                                    eh.wait_ge(sems[key], val)
                                ins = op["fn"](eh)
                                if emb is not None:
                                    ins._wait_ge(sems[emb[0]], emb[1])
                                if op["dma"]:
                                    ins.then_inc(sems[op["sig"][0]], op["inc"])
                                elif op["needed"]:
                                    ins.then_inc(sems[op["sig"][0]], 1)
                            if e == "sp" and last:
                                for ch, c in self.chan_count.items():
                                    key = ("dma", ch)
                                    if seen.get(key, 0) < self.chan_inc[ch] * c:
                                        eh.wait_ge(sems[key], self.chan_inc[ch] * c)

                        if oplist or (e == "sp" and last):
                            getattr(block, BLOCKNAME[e])(body)


EPS = 1e-6
NTOK = 4096
TT = 512
C_URNN, C_UGELU, C_CQ, C_CKV, C_FQ, C_FK, C_FV, C_FL, C_GATE = 0, 512, 1024, 1408, 1696, 2208, 2720, 3232, 3240
D_IN = 6312


def rmsnorm_T(S, nc, es, x_src, gcol, hT, pfx, ntok=NTOK, D=1024):
    KC = D // 128
    xt = [es.enter_context(nc.sbuf_tensor(f"{pfx}_xt{i}", [128, KC, TT], F32)) for i in range(2)]
    sq = es.enter_context(nc.sbuf_tensor(f"{pfx}_sq", [128, KC, TT], BF16))
    rstd = es.enter_context(nc.sbuf_tensor(f"{pfx}_rstd", [128, TT], F32))
    onesm = es.enter_context(nc.sbuf_tensor(f"{pfx}_ones", [128, 128], BF16))
    ps = es.enter_context(nc.psum_tensor(f"{pfx}_ps", [128, TT], F32))
    S.add("pool", lambda e: e.memset(onesm[:], 1.0 / D), writes=[(pfx, "ones")])
    xv = x_src.rearrange("(k p) n -> p k n", p=128)
    for t in range(ntok // TT):
        b = t % 2
        S.add("sp", lambda e, b=b, t=t: e.dma_start(out=xt[b][:], in_=xv[:, :, t * TT:(t + 1) * TT]),
              writes=[(pfx, "xt", b)], dma=True, chan=(pfx, "xt", b))
        S.add("act", lambda e, b=b: e.activation(out=sq[:], in_=xt[b][:], func=AF.Square),
              reads=[(pfx, "xt", b)], writes=[(pfx, "sq")])
        for k in range(KC):
            S.add("pe", lambda e, k=k: e.matmul(ps[:], lhsT=onesm[:], rhs=sq[:, k, :], start=(k == 0), stop=(k == KC - 1)),
                  reads=[(pfx, "sq"), (pfx, "ones")], writes=[(pfx, "ps")])
        S.add("act", lambda e: e.activation(out=rstd[:], in_=ps[:], func=AF.Sqrt, bias=EPS, scale=1.0),
              reads=[(pfx, "ps")], writes=[(pfx, "rstd")])
        S.add("dve", lambda e: e.reciprocal(out=rstd[:], in_=rstd[:]),
              reads=[(pfx, "rstd")], writes=[(pfx, "rstd")])
        for k in range(KC):
            S.add("dve", lambda e, k=k, b=b, t=t: e.scalar_tensor_tensor(
                out=hT[:, k, t * TT:(t + 1) * TT], in0=xt[b][:, k, :], scalar=gcol[:, k:k + 1], in1=rstd[:],
                op0=ALU.mult, op1=ALU.mult),
                reads=[(pfx, "xt", b), (pfx, "rstd"), "gcols"], writes=[("hT", t)])


def linear_T(S, nc, es, pfx, w_src, K, cols, rhs_fn, rhs_res_fn, ntok, evac_fn, wdt=BF16, GW=512):
    KC = K // 128
    wv = w_src.rearrange("(k p) m -> p k m", p=128)
    groups = []
    cur = []
    for ci, (c0, n) in enumerate(cols):
        if cur and (cur[0][1] + GW < c0 + n or cur[-1][1] + cur[-1][2] != c0):
            groups.append(cur)
            cur = []
        cur.append((ci, c0, n))
    if cur:
        groups.append(cur)
    wb = [es.enter_context(nc.sbuf_tensor(f"{pfx}_w{i}", [128, KC, GW], wdt)) for i in range(2)]
    pss = [es.enter_context(nc.psum_tensor(f"{pfx}_ps{i}", [128, TT], F32)) for i in range(2)]
    pi = 0
    for gi, g in enumerate(groups):
        b = gi % 2
        g0 = g[0][1]
        gn = g[-1][1] + g[-1][2] - g0
        S.add("pool", lambda e, b=b, g0=g0, gn=gn: e.dma_start(out=wb[b][:, :, 0:gn], in_=wv[:, :, g0:g0 + gn]),
              writes=[(pfx, "w", b)], dma=True, chan=(pfx, "w", b))
        for (ci, c0, n) in g:
            for t in range(ntok // TT):
                p = pi % 2
                pi += 1
                for k in range(KC):
                    S.add("pe", lambda e, b=b, p=p, k=k, t=t, c0=c0, n=n, g0=g0: e.matmul(
                        pss[p][0:n, :], lhsT=wb[b][:, k, c0 - g0:c0 - g0 + n], rhs=rhs_fn(k, t),
                        start=(k == 0), stop=(k == KC - 1)),
                        reads=[(pfx, "w", b)] + rhs_res_fn(k, t), writes=[(pfx, "ps", p)])
                evac_fn(ci, t, pss[p], (pfx, "ps", p), n)


def phase1(S, nc, xres, w_in_l, mixg_l, outs):
    with ExitStack() as es:
        hT = es.enter_context(nc.sbuf_tensor("p1_hT", [128, 8, NTOK], BF16))
        gcol = es.enter_context(nc.sbuf_tensor("p1_gcol", [128, 8], F32))
        S.add("sp", lambda e: e.dma_start(out=gcol[:], in_=mixg_l.rearrange("(k p) -> p k", p=128), allow_slow_non_contiguous=True),
              writes=["gcols"], dma=True, chan="gcols")
        with ExitStack() as es2:
            rmsnorm_T(S, nc, es2, xres, gcol, hT, "p1n")
        S.barrier()
        stage32 = [es.enter_context(nc.sbuf_tensor(f"p1_st32_{i}", [128, NTOK], F32)) for i in range(2)]
        stage16 = [es.enter_context(nc.sbuf_tensor(f"p1_st16_{i}", [128, NTOK], BF16)) for i in range(2)]
        tiles = []
        for j in range(8):
            tiles.append((j * 128, 128, outs["uT"], j * 128, F32))
        for j in range(3):
            tiles.append((C_CQ + j * 128, 128, outs["cqT"], j * 128, F32))
        for j in range(2):
            tiles.append((C_CKV + j * 128, 128, outs["ckvT"], j * 128, F32))
        tiles.append((C_CKV + 256, 32, outs["ckvT"], 256, F32))
        for j in range(4):
            tiles.append((C_FQ + j * 128, 128, outs["fqT"], j * 128, BF16))
        for j in range(4):
            tiles.append((C_FK + j * 128, 128, outs["fkT"], j * 128, BF16))
        tiles.append((C_FL, 8, outs["flT"], 0, F32))
        cols = [(c0, n) for (c0, n, _, _, _) in tiles]
        nT = NTOK // TT

        def evac(ci, t, ps, ps_res, rows):
            c0, n, dst, r0, dt = tiles[ci]
            sb_i = ci % 2
            st = stage32[sb_i] if dt == F32 else stage16[sb_i]
            view = st[0:n, t * TT:(t + 1) * TT]
            eng = "act" if (t % 2 == 0) else "dve"
            if eng == "act":
                S.add("act", lambda e: e.copy(out=view, in_=ps[0:n, :]), reads=[ps_res], writes=[("p1st", dt == F32, sb_i, t)])
            else:
                S.add("dve", lambda e: e.tensor_copy(out=view, in_=ps[0:n, :]), reads=[ps_res], writes=[("p1st", dt == F32, sb_i, t)])
            if t == nT - 1:
                src = st[0:n, :]
                S.add("sp", lambda e: e.dma_start(out=dst[r0:r0 + n, :], in_=src),
                      reads=[("p1st", dt == F32, sb_i, tt) for tt in range(nT)], writes=[("p1out", ci)], dma=True, chan=("p1st", dt == F32, sb_i))

        linear_T(S, nc, es, "p1l", w_in_l, 1024, cols, lambda k, t: hT[:, k, t * TT:(t + 1) * TT],
                 lambda k, t: [("hT", t)], NTOK, evac)
        wfv = es.enter_context(nc.sbuf_tensor("p1_wfv", [128, 8, 512], BF16))
        wv = w_in_l.rearrange("(k p) m -> p k m", p=128)
        S.add("pool", lambda e: e.dma_start(out=wfv[:], in_=wv[:, :, C_FV:C_FV + 512]), writes=["wfv"], dma=True, chan="wfv")
        psv = [es.enter_context(nc.psum_tensor(f"p1_psv{i}", [128, 512], F32)) for i in range(2)]
        stv = [es.enter_context(nc.sbuf_tensor(f"p1_stv{i}", [128, 4, 512], BF16)) for i in range(2)]
        fvv = outs["fv"].rearrange("(b p) c -> p b c", p=128)
        for tb in range(NTOK // 128):
            p = tb % 2
            sgi = (tb // 4) % 2
            for k in range(8):
                S.add("pe", lambda e, p=p, k=k, tb=tb: e.matmul(psv[p][:], lhsT=hT[:, k, tb * 128:(tb + 1) * 128], rhs=wfv[:, k, :],
                                                                 start=(k == 0), stop=(k == 7)),
                      reads=["wfv", ("hT", tb // 4)], writes=[("psv", p)])
            if tb % 2 == 0:
                S.add("act", lambda e, p=p, sgi=sgi, tb=tb: e.copy(out=stv[sgi][:, tb % 4, :], in_=psv[p][:]),
                      reads=[("psv", p)], writes=[("stv", sgi, tb % 4)])
            else:
                S.add("dve", lambda e, p=p, sgi=sgi, tb=tb: e.tensor_copy(out=stv[sgi][:, tb % 4, :], in_=psv[p][:]),
                      reads=[("psv", p)], writes=[("stv", sgi, tb % 4)])
            if tb % 4 == 3:
                b0 = tb - 3
                S.add("sp", lambda e, sgi=sgi, b0=b0: e.dma_start(out=fvv[:, b0:b0 + 4, :], in_=stv[sgi][:]),
                      reads=[("stv", sgi, q) for q in range(4)], writes=[("fvout", tb)], dma=True, chan=("stv", sgi))
    S.barrier()


ST = 1024
NTT = ST // TT
D_FF = 2816


class PsPool:
    def __init__(self, S, nc, es, n, pfx):
        self.S = S
        self.t = [es.enter_context(nc.psum_tensor(f"{pfx}_ps{i}", [128, TT], F32)) for i in range(n)]
        self.pfx = pfx
        self.i = 0

    def group(self, rows, mms, ncols=TT):
        i = self.i % len(self.t)
        self.i += 1
        ps = self.t[i]
        res = (self.pfx, "ps", i)
        n = len(mms)
        for q, (l, r, rd) in enumerate(mms):
            self.S.add("pe", lambda e, l=l, r=r, q=q: e.matmul(ps[0:rows, 0:ncols], lhsT=l, rhs=r, start=(q == 0), stop=(q == n - 1)),
                       reads=list(rd), writes=[res])
        return ps[0:rows, 0:ncols], res


def norm_sb(S, nc, P, src_fn, src_res_fn, gcol, gres, dst_fn, dst_res_fn, sq, rstd, onesm, nt, KC=8):
    for t in range(nt):
        for k in range(KC):
            S.add("act", lambda e, k=k, t=t: e.activation(out=sq[:, k, :], in_=src_fn(k, t), func=AF.Square),
                  reads=[src_res_fn(k, t)], writes=[("sq", k)])
        ps, pres = P.group(128, [(onesm[:], sq[:, k, :], [("sq", k), "ones"]) for k in range(KC)])
        S.add("act", lambda e, ps=ps: e.activation(out=rstd[:], in_=ps, func=AF.Sqrt, bias=EPS, scale=1.0),
              reads=[pres], writes=["rstd"])
        S.add("dve", lambda e: e.reciprocal(out=rstd[:], in_=rstd[:]), reads=["rstd"], writes=["rstd"])
        for k in range(KC):
            S.add("dve", lambda e, k=k, t=t: e.scalar_tensor_tensor(
                out=dst_fn(k, t), in0=src_fn(k, t), scalar=gcol[:, k:k + 1], in1=rstd[:], op0=ALU.mult, op1=ALU.mult),
                reads=[src_res_fn(k, t), "rstd", gres], writes=[dst_res_fn(k, t)])


def phase3(S, nc, xres_in, xres_out, brT, pT_l, W, final_out=None, stop=9, wq="pool"):
    with ExitStack() as es:
        sb = lambda name, shape, dt: es.enter_context(nc.sbuf_tensor("p3_" + name, shape, dt))
        x1T = sb("x1T", [128, 8, ST], F32)
        sq = sb("sq", [128, 8, TT], BF16)
        rstd = sb("rstd", [128, TT], F32)
        onesm = sb("ones", [128, 128], BF16)
        hT = sb("hT", [128, 8, ST], BF16)
        brS = sb("brS", [128, 3, 4, ST], BF16)
        mgT = sb("mgT", [128, 8, ST], BF16)
        actT = sb("actT", [128, 11, ST], BF16)
        pS = sb("pS", [128, 2, ST], BF16)
        wsm = [sb(f"wsm{i}", [128, 3, 12, 128], BF16) for i in range(2)]
        wst = [sb(f"wst{i}", [128, 11, 512], BF16) for i in range(2)]
        gs = [sb(f"gs{i}", [128, TT], F32) for i in range(3)]
        tm = [sb(f"tm{i}", [128, TT], F32) for i in range(3)]
        vec = sb("vec", [128, 24 + 8 * 4], F32)
        P = PsPool(S, nc, es, 7, "p3")

        S.add("pool", lambda e: e.memset(onesm[:], 1.0 / 1024), writes=["ones"])
        S.add("sp", lambda e: e.dma_start(out=vec[:, 0:24], in_=W["gate_b"].rearrange("(j p) -> p j", p=128), allow_slow_non_contiguous=True),
              writes=["vec"], dma=True, chan="vec")
        for i, nm in enumerate(["mixg", "ffng", "pleg", "fing"]):
            if nm in W:
                S.add("sp", lambda e, i=i, nm=nm: e.dma_start(out=vec[:, 24 + 8 * i:32 + 8 * i], in_=W[nm].rearrange("(k p) -> p k", p=128),
                                                               allow_slow_non_contiguous=True), writes=["vec"], dma=True, chan="vec")
        gb = vec[:, 0:24]
        gmix, gffn, gple, gfin = (vec[:, 24 + 8 * i:32 + 8 * i] for i in range(4))
        xin_v = xres_in.rearrange("(k p) n -> p k n", p=128)
        pv = pT_l.rearrange("(k p) n -> p k n", p=128)
        wsi = 0
        wti = 0

        def x1res(k, t):
            return ("x1", k, t)

        for s in range(NTOK // ST):
            tok0 = s * ST
            for t in range(NTT):
                S.add("sp", lambda e, t=t, tok0=tok0: e.dma_start(out=x1T[:, :, t * TT:(t + 1) * TT], in_=xin_v[:, :, tok0 + t * TT:tok0 + (t + 1) * TT]),
                      writes=[x1res(k, t) for k in range(8)], dma=True, chan=("x1ld", t))
            for j in range(3):
                S.add("sp", lambda e, j=j, tok0=tok0: e.dma_start(out=brS[:, j, :, :], in_=brT[j].rearrange("(k p) n -> p k n", p=128)[:, :, tok0:tok0 + ST]),
                      reads=[("brin", j, 0), ("brin", j, 1)], writes=[("br", j)], dma=True, chan=("br", j))
            S.add("pool", lambda e, tok0=tok0: e.dma_start(out=pS[:], in_=pv[:, :, tok0:tok0 + ST]), writes=["pS"], dma=True, chan="pS")
            norm_sb(S, nc, P, lambda k, t: x1T[:, k, t * TT:(t + 1) * TT], x1res, gmix, "vec",
                    lambda k, t: hT[:, k, t * TT:(t + 1) * TT], lambda k, t: ("hT", k, t), sq, rstd, onesm, NTT)
            for m in range(8):
                b = wsi % 2
                wsi += 1
                for j in range(3):
                    S.add(wq, lambda e, b=b, j=j, m=m: e.dma_start(
                        out=wsm[b][:, j, 0:4, :], in_=W["w_br"][j].rearrange("(k p) c -> p k c", p=128)[:, :, m * 128:(m + 1) * 128]),
                        writes=[("wsm", b, j, 0)], dma=True, chan=("wsm", b))
                    gc = (0 if "w_gate" in W else C_GATE) + j * 1024 + m * 128
                    S.add(wq, lambda e, b=b, j=j, gc=gc: e.dma_start(
                        out=wsm[b][:, j, 4:12, :], in_=W.get("w_gate", W["w_in"]).rearrange("(k p) c -> p k c", p=128)[:, :, gc:gc + 128]),
                        writes=[("wsm", b, j, 1)], dma=True, chan=("wsm", b))
                for t in range(NTT):
                    tsl = slice(t * TT, (t + 1) * TT)
                    for j in range(3):
                        psg, rg = P.group(128, [(wsm[b][:, j, 4 + k, :], hT[:, k, tsl], [("wsm", b, j, 1), ("hT", k, t)]) for k in range(8)])
                        psy, ry = P.group(128, [(wsm[b][:, j, k, :], brS[:, j, k, tsl], [("wsm", b, j, 0), ("br", j)]) for k in range(4)])
                        S.add("act", lambda e, j=j, m=m, psg=psg: e.activation(out=gs[j][:], in_=psg, func=AF.Sigmoid, bias=gb[:, j * 8 + m:j * 8 + m + 1], scale=1.0),
                              reads=[rg, "vec"], writes=[("gs", j)])
                        S.add("dve", lambda e, j=j, psy=psy: e.tensor_tensor(out=tm[j][:], in0=psy, in1=gs[j][:], op=ALU.mult),
                              reads=[ry, ("gs", j)], writes=[("tm", j)])
                    S.add("pool", lambda e: e.tensor_tensor(out=tm[0][:], in0=tm[0][:], in1=tm[1][:], op=ALU.add),
                          reads=[("tm", 0), ("tm", 1)], writes=[("tm", 0)])
                    S.add("pool", lambda e, m=m, tsl=tsl: e.tensor_tensor(out=mgT[:, m, tsl], in0=tm[0][:], in1=tm[2][:], op=ALU.add),
                          reads=[("tm", 0), ("tm", 2)], writes=[("mg", m, t)])

            def lin(wsrc, KC, ncols, rhs_fn, rhs_res_fn, evac, row0=0):
                nonlocal wti
                wv_ = wsrc.rearrange("(k p) c -> p k c", p=128)
                k0 = row0 // 128
                for g0 in range(0, ncols, 512):
                    gn = min(512, ncols - g0)
                    b_ = wti % 2
                    wti += 1
                    S.add(wq, lambda e, b_=b_, g0=g0, gn=gn: e.dma_start(out=wst[b_][:, 0:KC, 0:gn], in_=wv_[:, k0:k0 + KC, g0:g0 + gn]),
                          writes=[("wst", b_)], dma=True, chan=("wst", b_))
                    for c in range(gn // 128):
                        m_ = (g0 // 128) + c
                        for t in range(NTT):
                            ps, r = P.group(128, [(wst[b_][:, k, c * 128:(c + 1) * 128], rhs_fn(k, t), [("wst", b_), rhs_res_fn(k, t)]) for k in range(KC)])
                            evac(m_, t, ps, r)

            def add_into_x1(m_, t, ps, r):
                tsl = slice(t * TT, (t + 1) * TT)
                S.add("dve", lambda e: e.tensor_tensor(out=x1T[:, m_, tsl], in0=ps, in1=x1T[:, m_, tsl], op=ALU.add),
                      reads=[r, x1res(m_, t)], writes=[x1res(m_, t)])

            if stop >= 1:
                lin(W["w_o"], 8, 1024, lambda k, t: mgT[:, k, t * TT:(t + 1) * TT], lambda k, t: ("mg", k, t), add_into_x1)
            if stop >= 2:
                norm_sb(S, nc, P, lambda k, t: x1T[:, k, t * TT:(t + 1) * TT], x1res, gffn, "vec",
                        lambda k, t: hT[:, k, t * TT:(t + 1) * TT], lambda k, t: ("hT", k, t), sq, rstd, onesm, NTT)
                gu_v = W["w_gu"].rearrange("(k p) c -> p k c", p=128)
                for half in range(2):
                    h0 = half * (D_FF // 2)
                    for g0 in range(0, D_FF // 2, 256):
                        gn = min(256, D_FF // 2 - g0)
                        b_ = wti % 2
                        wti += 1
                        S.add(wq, lambda e, b_=b_, g0=g0, gn=gn, h0=h0: e.dma_start(out=wst[b_][:, 0:8, 0:gn], in_=gu_v[:, :, h0 + g0:h0 + g0 + gn]),
                              writes=[("wst", b_), ("wst", b_, 0)], dma=True, chan=("wst", b_))
                        S.add(wq, lambda e, b_=b_, g0=g0, gn=gn, h0=h0: e.dma_start(out=wst[b_][:, 0:8, 256:256 + gn], in_=gu_v[:, :, D_FF + h0 + g0:D_FF + h0 + g0 + gn]),
                              writes=[("wst", b_, 1)], dma=True, chan=("wst", b_))
                        for c in range(gn // 128):
                            ci = g0 // 128 + c
                            for t in range(NTT):
                                tsl = slice(t * TT, (t + 1) * TT)
                                psg, rg = P.group(128, [(wst[b_][:, k, c * 128:(c + 1) * 128], hT[:, k, tsl], [("wst", b_), ("wst", b_, 0), ("hT", k, t)]) for k in range(8)])
                                psu, ru = P.group(128, [(wst[b_][:, k, 256 + c * 128:256 + (c + 1) * 128], hT[:, k, tsl], [("wst", b_), ("wst", b_, 1), ("hT", k, t)]) for k in range(8)])
                                q = (ci * NTT + t) % 3
                                S.add("act", lambda e, q=q, psg=psg: e.activation(out=gs[q][:], in_=psg, func=AF.Silu), reads=[rg], writes=[("gs", q)])
                                S.add("dve", lambda e, q=q, psu=psu, ci=ci, tsl=tsl: e.tensor_tensor(out=actT[:, ci, tsl], in0=psu, in1=gs[q][:], op=ALU.mult),
                                      reads=[ru, ("gs", q)], writes=[("act", ci, t)])
                    lin(W["w_down"], 11, 1024, lambda k, t: actT[:, k, t * TT:(t + 1) * TT], lambda k, t: ("act", k, t), add_into_x1, row0=h0)
            if stop >= 3:
                norm_sb(S, nc, P, lambda k, t: x1T[:, k, t * TT:(t + 1) * TT], x1res, gple, "vec",
                        lambda k, t: hT[:, k, t * TT:(t + 1) * TT], lambda k, t: ("hT", k, t), sq, rstd, onesm, NTT)
                pg_v = W["w_pg"].rearrange("(k p) c -> p k c", p=128)
                pe_v = W["w_ple"].rearrange("(k p) c -> p k c", p=128)
                for g0 in range(0, 1024, 256):
                    b_ = wti % 2
                    wti += 1
                    S.add(wq, lambda e, b_=b_, g0=g0: e.dma_start(out=wst[b_][:, 0:8, 0:256], in_=pg_v[:, :, g0:g0 + 256]),
                          writes=[("wst", b_), ("wst", b_, 0)], dma=True, chan=("wst", b_))
                    S.add(wq, lambda e, b_=b_, g0=g0: e.dma_start(out=wst[b_][:, 0:2, 256:512], in_=pe_v[:, :, g0:g0 + 256]),
                          writes=[("wst", b_, 1)], dma=True, chan=("wst", b_))
                    for c in range(2):
                        m_ = g0 // 128 + c
                        for t in range(NTT):
                            tsl = slice(t * TT, (t + 1) * TT)
                            psg, rg = P.group(128, [(wst[b_][:, k, c * 128:(c + 1) * 128], hT[:, k, tsl], [("wst", b_), ("wst", b_, 0), ("hT", k, t)]) for k in range(8)])
                            pse, re_ = P.group(128, [(wst[b_][:, k, 256 + c * 128:256 + (c + 1) * 128], pS[:, k, tsl], [("wst", b_), ("wst", b_, 1), "pS"]) for k in range(2)])
                            q = (m_ * NTT + t) % 3
                            S.add("act", lambda e, q=q, psg=psg: e.activation(out=gs[q][:], in_=psg, func=AF.Sigmoid), reads=[rg], writes=[("gs", q)])
                            S.add("dve", lambda e, q=q, pse=pse: e.tensor_tensor(out=tm[q][:], in0=pse, in1=gs[q][:], op=ALU.mult),
                                  reads=[re_, ("gs", q)], writes=[("tm", q)])
                            S.add("dve", lambda e, q=q, m_=m_, tsl=tsl: e.tensor_tensor(out=x1T[:, m_, tsl], in0=tm[q][:], in1=x1T[:, m_, tsl], op=ALU.add),
                                  reads=[("tm", q), x1res(m_, t)], writes=[x1res(m_, t)])
            if final_out is None:
                xo_v = xres_out.rearrange("(k p) n -> p k n", p=128)
                for t in range(NTT):
                    S.add("sp", lambda e, t=t, tok0=tok0: e.dma_start(out=xo_v[:, :, tok0 + t * TT:tok0 + (t + 1) * TT], in_=x1T[:, :, t * TT:(t + 1) * TT]),
                          reads=[x1res(k, t) for k in range(8)], writes=[("xout", s, t)], dma=True, chan=("x1ld", t))
            else:
                fo_v = final_out.rearrange("(k p) n -> p k n", p=128)
                norm_sb(S, nc, P, lambda k, t: x1T[:, k, t * TT:(t + 1) * TT], x1res, gfin, "vec",
                        lambda k, t: x1T[:, k, t * TT:(t + 1) * TT], x1res, sq, rstd, onesm, NTT)
                for t in range(NTT):
                    S.add("sp", lambda e, t=t, tok0=tok0: e.dma_start(out=fo_v[:, :, tok0 + t * TT:tok0 + (t + 1) * TT], in_=x1T[:, :, t * TT:(t + 1) * TT]),
                          reads=[x1res(k, t) for k in range(8)], writes=[("xout", s, t)], dma=True, chan=("x1ld", t))
    S.barrier()


SEQ = 8192
HALF = 4096
NKB = SEQ // 128
NEG = -30000.0
TL = 2048


def hsl(ap3, r0, r1, t0, n):
    h = t0 // HALF
    assert (t0 + n - 1) // HALF == h
    return ap3[h, r0:r1, t0 - h * HALF:t0 - h * HALF + n]


def phase2(S, nc, I, W, C, brT_out, scr, hook_c=None, hook_mid=None, hook_d=None):
    IR = I.get("res", {})
    with ExitStack() as es0:
        psum = [es0.enter_context(nc.psum_tensor(f"p2_ps{i}", [128, TT], F32)) for i in range(8)]
        identf = es0.enter_context(nc.sbuf_tensor("p2_identf", [128, 128], F32))
        identb = es0.enter_context(nc.sbuf_tensor("p2_identb", [128, 128], BF16))
        rampF = es0.enter_context(nc.sbuf_tensor("p2_rampF", [128, 896], BF16))
        rampM = es0.enter_context(nc.sbuf_tensor("p2_rampM", [128, 896], BF16))
        ncumT = es0.enter_context(nc.sbuf_tensor("p2_ncumT", [128, NKB * 4], F32))
        S.add("sp", lambda e: e.dma_start(out=identf[:], in_=C["ident"]), writes=["identf"], dma=True, chan="identf")
        S.add("pool", lambda e: e.dma_start(out=identb[:], in_=C["ident"]), writes=["identb"], dma=True, chan="identb")
        S.add("pool", lambda e: e.dma_start(out=rampF[:], in_=C["ramp_fox"]), writes=["rampF"], dma=True, chan="rampF")
        S.add("pool", lambda e: e.dma_start(out=rampM[:], in_=C["ramp_mla"]), writes=["rampM"], dma=True, chan="rampM")

        with ExitStack() as es:
            sb = lambda n, s, d: es.enter_context(nc.sbuf_tensor("p2a_" + n, s, d))
            fl = sb("fl", [4, SEQ], F32)
            lg = sb("lg", [4, SEQ], F32)
            onesr = sb("onesr", [4, SEQ], F32)
            ncum = sb("ncum", [4, SEQ], F32)
            qa = sb("qa", [4, SEQ], BF16)
            bfc = sb("bfc", [4, 1], F32)
            for h in range(2):
                S.add("sp", lambda e, h=h: e.dma_start(out=fl[:, h * HALF:(h + 1) * HALF], in_=I["flT2"][h]), reads=IR.get("flT2", []), writes=[("fl", h)], dma=True, chan=("fl", h))
            S.add("sp", lambda e: e.dma_start(out=bfc[:], in_=W["bf"].rearrange("(p o) -> p o", o=1)), writes=["bfc"], dma=True, chan="bfc")
            S.add("dve", lambda e: e.tensor_scalar_mul(out=bfc[:], in0=bfc[:], scalar1=-1.0), reads=["bfc"], writes=["bfc"])
            S.add("pool", lambda e: e.memset(onesr[:], 1.0), writes=["onesr"])
            S.add("act", lambda e: e.activation(out=lg[:], in_=fl[:], func=AF.Exp, bias=bfc[:, 0:1], scale=-1.0),
                  reads=[("fl", 0), ("fl", 1), "bfc"], writes=["lg"])
            S.add("act", lambda e: e.activation(out=lg[:], in_=lg[:], func=AF.Ln, bias=1.0, scale=1.0), reads=["lg"], writes=["lg"])
            S.add("dve", lambda e: e.tensor_tensor_scan(out=ncum[:], data0=onesr[:], data1=lg[:], initial=0.0, op0=ALU.mult, op1=ALU.add),
                  reads=["lg", "onesr"], writes=["ncum"])
            S.add("dve", lambda e: e.tensor_scalar_mul(out=qa[:], in0=ncum[:], scalar1=-8.0), reads=["ncum"], writes=["qa"])
            S.add("sp", lambda e: e.dma_start(out=scr["caug"], in_=qa[:]), reads=["qa"], writes=["caug"], dma=True, chan="caug")
            for blk in range(NKB):
                S.add("pe", lambda e, blk=blk: e.transpose(out=psum[0][:, blk * 4:(blk + 1) * 4], in_=ncum[0:4, blk * 128:(blk + 1) * 128], identity=identf[0:4, 0:4]),
                      reads=["ncum", "identf"], writes=[("ps", 0)])
            S.add("dve", lambda e: e.tensor_copy(out=ncumT[:], in_=psum[0][:, 0:NKB * 4]), reads=[("ps", 0)], writes=["ncumT"])
        S.barrier()

        with ExitStack() as es:
            sb = lambda n, s, d: es.enter_context(nc.sbuf_tensor("p2b_" + n, s, d))
            wuq = sb("wuq", [128, 3, 384], BF16)
            wuqS = sb("wuqS", [128, 3, 384], BF16)
            wuk = sb("wuk", [128, 2, 4, 64], BF16)
            wuv = sb("wuv", [128, 2, 4, 64], BF16)
            gq = sb("gq", [128, 3], F32)
            gkv = sb("gkv", [128, 2], F32)
            onesm = sb("ones", [128, 128], BF16)
            cq = [sb(f"cq{i}", [128, 3, TT], F32) for i in range(2)]
            ckv = [sb(f"ckv{i}", [128, 2, TT], F32) for i in range(2)]
            krA = [sb(f"krA{i}", [96, TT], F32) for i in range(2)]
            krB = [sb(f"krB{i}", [96, TT], F32) for i in range(2)]
            cs = [sb(f"cs{i}", [96, 2, TT], F32) for i in range(2)]
            sq2 = [sb(f"sq{i}", [128, 3, TT], BF16) for i in range(2)]
            rstd2 = [sb(f"rstd{i}", [128, TT], F32) for i in range(2)]
            cqn2 = [sb(f"cqn{i}", [128, 3, TT], BF16) for i in range(2)]
            ckvn2 = [sb(f"ckvn{i}", [128, 2, TT], BF16) for i in range(2)]
            t1s = [sb(f"t1_{i}", [96, TT], F32) for i in range(2)]
            t2s = [sb(f"t2_{i}", [96, TT], F32) for i in range(2)]
            qst = [sb(f"qst{i}", [96, TT], BF16) for i in range(2)]
            kst = [sb(f"kst{i}", [64, TT], BF16) for i in range(2)]
            krs = [sb(f"krs{i}", [96, TT], BF16) for i in range(2)]
            vst = [sb(f"vst{i}", [128, 4, 64], BF16) for i in range(2)]
            S.add("pool", lambda e: e.memset(onesm[:], 1.0), writes=["ones"])
            S.add("pool", lambda e: e.memset(wuqS[:], 0.0), writes=["wuqS"])
            wq_v = W["wuq"].rearrange("(k p) c -> p k c", p=128)
            S.add("pool", lambda e: e.dma_start(out=wuq[:], in_=wq_v), writes=["wuq"], dma=True, chan="wuq")
            for h in range(4):
                S.add("pool", lambda e, h=h: e.dma_start(out=wuqS[:, :, h * 96 + 64:h * 96 + 80], in_=wq_v[:, :, h * 96 + 80:h * 96 + 96]),
                      reads=["wuqS"], writes=[("wuqS", h, 0)], dma=True, chan="wuqS")
                S.add("pool", lambda e, h=h: e.dma_start(out=wuqS[:, :, h * 96 + 80:h * 96 + 96], in_=wq_v[:, :, h * 96 + 64:h * 96 + 80]),
                      reads=["wuqS"], writes=[("wuqS", h, 1)], dma=True, chan="wuqS")
            wkv_v = W["wukv"].rearrange("(k p) (h two d) -> p k h two d", p=128, two=2, d=64)
            for k in range(2):
                S.add("pool", lambda e, k=k: e.dma_start(out=wuk[:, k], in_=wkv_v[:, k, :, 0, :]), writes=[("wuk", k)], dma=True, chan="wuk")
                S.add("pool", lambda e, k=k: e.dma_start(out=wuv[:, k], in_=wkv_v[:, k, :, 1, :]), writes=[("wuv", k)], dma=True, chan="wuv")
            S.add("sp", lambda e: e.dma_start(out=gq[:], in_=W["qn"].rearrange("(k p) -> p k", p=128), allow_slow_non_contiguous=True), writes=["gq"], dma=True, chan="gq")
            S.add("sp", lambda e: e.dma_start(out=gkv[:], in_=W["kvn"].rearrange("(k p) -> p k", p=128), allow_slow_non_contiguous=True), writes=["gkv"], dma=True, chan="gkv")
            wuqS_res = ["wuqS"] + [("wuqS", h, i) for h in range(4) for i in range(2)]
            pi = 0

            def nxt():
                nonlocal pi
                pi += 1
                return 1 + (pi % 7)

            for tc in range(SEQ // TT):
                b = tc % 2
                t0 = tc * TT
                sq, rstd, cqn, ckvn, t1, t2 = sq2[b], rstd2[b], cqn2[b], ckvn2[b], t1s[b], t2s[b]
                if "cq_chunks" in I:
                    hh_, tl = t0 // HALF, t0 % HALF
                    src_cq = I["cq_chunks"][:, hh_, :, tl:tl + TT].rearrange("k p n -> p k n")
                    src_ckv = I["ckv_chunks"][0:2, hh_, :, tl:tl + TT].rearrange("k p n -> p k n")
                    kr = lambda r0, r1, hh_=hh_, tl=tl: I["ckv_chunks"][2, hh_, r0 - 256:r1 - 256, tl:tl + TT]
                    rq, rkv = [I["cq_res"]], [I["ckv_res"]]
                else:
                    src_cq = hsl(I["cqT2"], 0, 384, t0, TT).rearrange("(k p) n -> p k n", p=128)
                    src_ckv = hsl(I["ckvT2"], 0, 256, t0, TT).rearrange("(k p) n -> p k n", p=128)
                    kr = lambda r0, r1, t0=t0: hsl(I["ckvT2"], r0, r1, t0, TT)
                    rq, rkv = [], []
                S.add("sp", lambda e, b=b, src_cq=src_cq: e.dma_start(out=cq[b][:], in_=src_cq), reads=rq, writes=[("cq", b)], dma=True, chan=("cq", b))
                S.add("sp", lambda e, b=b, src_ckv=src_ckv: e.dma_start(out=ckv[b][:], in_=src_ckv), reads=rkv, writes=[("ckv", b)], dma=True, chan=("ckv", b))
                S.add("sp", lambda e, b=b, a_=kr(256, 288): e.dma_start(out=krA[b][64:96, :], in_=a_), reads=rkv, writes=[("krA", b)], dma=True, chan=("krA", b))
                S.add("sp", lambda e, b=b, a_=kr(272, 288): e.dma_start(out=krB[b][64:80, :], in_=a_), reads=rkv, writes=[("krB", b, 0)], dma=True, chan=("krB", b))
                S.add("sp", lambda e, b=b, a_=kr(256, 272): e.dma_start(out=krB[b][80:96, :], in_=a_), reads=rkv, writes=[("krB", b, 1)], dma=True, chan=("krB", b))
                S.add("sp", lambda e, sq=sq, rstd=rstd, cqn=cqn, ckvn=ckvn, t1=t1, t2=t2, b=b, t0=t0: e.dma_start(out=cs[b][64:96, 0, :], in_=C["cosT"][:, t0:t0 + TT]), writes=[("cs", b, 0)], dma=True, chan=("cs", b))
                S.add("sp", lambda e, sq=sq, rstd=rstd, cqn=cqn, ckvn=ckvn, t1=t1, t2=t2, b=b, t0=t0: e.dma_start(out=cs[b][64:96, 1, :], in_=C["sinS"][:, t0:t0 + TT]), writes=[("cs", b, 1)], dma=True, chan=("cs", b))
                for (src, KC, dst, g, D, nm) in ((cq[b], 3, cqn, gq, 384, "cq"), (ckv[b], 2, ckvn, gkv, 256, "ckv")):
                    S.add("act", lambda e, sq=sq, rstd=rstd, cqn=cqn, ckvn=ckvn, t1=t1, t2=t2, src=src, KC=KC: e.activation(out=sq[:, 0:KC, :], in_=src[:], func=AF.Square), reads=[(nm, b)], writes=[("sq", b)])
                    pn = nxt()
                    for k in range(KC):
                        S.add("pe", lambda e, sq=sq, rstd=rstd, cqn=cqn, ckvn=ckvn, t1=t1, t2=t2, k=k, KC=KC, pn=pn: e.matmul(psum[pn][:], lhsT=onesm[:], rhs=sq[:, k, :], start=(k == 0), stop=(k == KC - 1)),
                              reads=[("sq", b), "ones"], writes=[("ps", pn)])
                    S.add("act", lambda e, sq=sq, rstd=rstd, cqn=cqn, ckvn=ckvn, t1=t1, t2=t2, pn=pn, D=D: e.activation(out=rstd[:], in_=psum[pn][:], func=AF.Sqrt, bias=EPS, scale=1.0 / D), reads=[("ps", pn)], writes=[("rstd", b)])
                    S.add("dve", lambda e, sq=sq, rstd=rstd, cqn=cqn, ckvn=ckvn, t1=t1, t2=t2: e.reciprocal(out=rstd[:], in_=rstd[:]), reads=[("rstd", b)], writes=[("rstd", b)])
                    for k in range(KC):
                        S.add("dve", lambda e, sq=sq, rstd=rstd, cqn=cqn, ckvn=ckvn, t1=t1, t2=t2, k=k, src=src, dst=dst, g=g: e.scalar_tensor_tensor(out=dst[:, k, :], in0=src[:, k, :], scalar=g[:, k:k + 1], in1=rstd[:],
                                                                                                op0=ALU.mult, op1=ALU.mult),
                              reads=[(nm, b), ("rstd", b), "gq", "gkv"], writes=[(nm + "n", b, k)])
                kb_ = tc % 2
                S.add("dve", lambda e, sq=sq, rstd=rstd, cqn=cqn, ckvn=ckvn, t1=t1, t2=t2, b=b: e.tensor_tensor(out=t1[64:96, :], in0=krA[b][64:96, :], in1=cs[b][64:96, 0, :], op=ALU.mult),
                      reads=[("krA", b), ("cs", b, 0)], writes=[("t1", b)])
                S.add("dve", lambda e, sq=sq, rstd=rstd, cqn=cqn, ckvn=ckvn, t1=t1, t2=t2, b=b: e.tensor_tensor(out=t2[64:96, :], in0=krB[b][64:96, :], in1=cs[b][64:96, 1, :], op=ALU.mult),
                      reads=[("krB", b, 0), ("krB", b, 1), ("cs", b, 1)], writes=[("t2", b)])
                S.add("dve", lambda e, sq=sq, rstd=rstd, cqn=cqn, ckvn=ckvn, t1=t1, t2=t2, kb_=kb_: e.tensor_tensor(out=krs[kb_][64:96, :], in0=t1[64:96, :], in1=t2[64:96, :], op=ALU.add),
                      reads=[("t1", b), ("t2", b)], writes=[("krs", kb_)])
                for h in range(4):
                    S.add("sp", lambda e, sq=sq, rstd=rstd, cqn=cqn, ckvn=ckvn, t1=t1, t2=t2, h=h, kb_=kb_, t0=t0: e.dma_start(out=scr["KT"][h, 64:96, t0:t0 + TT], in_=krs[kb_][64:96, :]),
                          reads=[("krs", kb_)], writes=[("KTr", h, tc)], dma=True, chan=("krs", kb_))
                for h in range(4):
                    qb = (tc * 4 + h) % 2
                    pa = nxt()
                    pb = nxt()
                    for k in range(3):
                        S.add("pe", lambda e, sq=sq, rstd=rstd, cqn=cqn, ckvn=ckvn, t1=t1, t2=t2, k=k, h=h, pa=pa: e.matmul(psum[pa][0:96, :], lhsT=wuq[:, k, h * 96:(h + 1) * 96], rhs=cqn[:, k, :], start=(k == 0), stop=(k == 2)),
                              reads=["wuq"] + [("cqn", b, kk) for kk in range(3)], writes=[("ps", pa)])
                    for k in range(3):
                        S.add("pe", lambda e, sq=sq, rstd=rstd, cqn=cqn, ckvn=ckvn, t1=t1, t2=t2, k=k, h=h, pb=pb: e.matmul(psum[pb][0:96, :], lhsT=wuqS[:, k, h * 96:(h + 1) * 96], rhs=cqn[:, k, :], start=(k == 0), stop=(k == 2)),
                              reads=wuqS_res + [("cqn", b, kk) for kk in range(3)], writes=[("ps", pb)])
                    S.add("act", lambda e, sq=sq, rstd=rstd, cqn=cqn, ckvn=ckvn, t1=t1, t2=t2, pa=pa, qb=qb: e.copy(out=qst[qb][0:64, :], in_=psum[pa][0:64, :]), reads=[("ps", pa)], writes=[("qst", qb, 0)])
                    S.add("dve", lambda e, sq=sq, rstd=rstd, cqn=cqn, ckvn=ckvn, t1=t1, t2=t2, pa=pa, b=b: e.tensor_tensor(out=t1[64:96, :], in0=psum[pa][64:96, :], in1=cs[b][64:96, 0, :], op=ALU.mult),
                          reads=[("ps", pa), ("cs", b, 0)], writes=[("t1", b)])
                    S.add("dve", lambda e, sq=sq, rstd=rstd, cqn=cqn, ckvn=ckvn, t1=t1, t2=t2, pb=pb, b=b: e.tensor_tensor(out=t2[64:96, :], in0=psum[pb][64:96, :], in1=cs[b][64:96, 1, :], op=ALU.mult),
                          reads=[("ps", pb), ("cs", b, 1)], writes=[("t2", b)])
                    S.add("dve", lambda e, sq=sq, rstd=rstd, cqn=cqn, ckvn=ckvn, t1=t1, t2=t2, qb=qb: e.tensor_tensor(out=qst[qb][64:96, :], in0=t1[64:96, :], in1=t2[64:96, :], op=ALU.add),
                          reads=[("t1", b), ("t2", b)], writes=[("qst", qb, 1)])
                    S.add("sp", lambda e, sq=sq, rstd=rstd, cqn=cqn, ckvn=ckvn, t1=t1, t2=t2, h=h, qb=qb, t0=t0: e.dma_start(out=scr["QT"][h, :, t0:t0 + TT], in_=qst[qb][:]),
                          reads=[("qst", qb, 0), ("qst", qb, 1)], writes=[("QT", h, tc)], dma=True, chan=("qst", qb))
                    pk = nxt()
                    for k in range(2):
                        S.add("pe", lambda e, sq=sq, rstd=rstd, cqn=cqn, ckvn=ckvn, t1=t1, t2=t2, k=k, h=h, pk=pk: e.matmul(psum[pk][0:64, :], lhsT=wuk[:, k, h, :], rhs=ckvn[:, k, :], start=(k == 0), stop=(k == 1)),
                              reads=[("wuk", 0), ("wuk", 1), ("ckvn", b, 0), ("ckvn", b, 1)], writes=[("ps", pk)])
                    S.add("act", lambda e, sq=sq, rstd=rstd, cqn=cqn, ckvn=ckvn, t1=t1, t2=t2, pk=pk, qb=qb: e.copy(out=kst[qb][:], in_=psum[pk][0:64, :]), reads=[("ps", pk)], writes=[("kst", qb)])
                    S.add("sp", lambda e, sq=sq, rstd=rstd, cqn=cqn, ckvn=ckvn, t1=t1, t2=t2, h=h, qb=qb, t0=t0: e.dma_start(out=scr["KT"][h, 0:64, t0:t0 + TT], in_=kst[qb][:]),
                          reads=[("kst", qb)], writes=[("KTn", h, tc)], dma=True, chan=("kst", qb))
                for tb in range(4):
                    vb = (tc * 4 + tb) % 2
                    pv = nxt()
                    for k in range(2):
                        S.add("pe", lambda e, sq=sq, rstd=rstd, cqn=cqn, ckvn=ckvn, t1=t1, t2=t2, k=k, tb=tb, pv=pv: e.matmul(psum[pv][:, 0:256], lhsT=ckvn[:, k, tb * 128:(tb + 1) * 128], rhs=wuv[:, k].rearrange("p h d -> p (h d)"),
                                                                         start=(k == 0), stop=(k == 1)),
                              reads=[("wuv", 0), ("wuv", 1), ("ckvn", b, 0), ("ckvn", b, 1)], writes=[("ps", pv)])
                    S.add("act", lambda e, sq=sq, rstd=rstd, cqn=cqn, ckvn=ckvn, t1=t1, t2=t2, pv=pv, vb=vb: e.copy(out=vst[vb][:].rearrange("p h d -> p (h d)"), in_=psum[pv][:, 0:256]), reads=[("ps", pv)], writes=[("vst", vb)])
                    tok = t0 + tb * 128
                    S.add("sp", lambda e, sq=sq, rstd=rstd, cqn=cqn, ckvn=ckvn, t1=t1, t2=t2, vb=vb, tok=tok: e.dma_start(out=scr["Vd"][:, tok:tok + 128, :].rearrange("h p d -> p h d"), in_=vst[vb][:]),
                          reads=[("vst", vb)], writes=[("Vd", tok // 128)], dma=True, chan=("vst", vb))
        S.barrier()

        if hook_c is not None:
            hook_c()
        with ExitStack() as es:
            sb = lambda n, s, d: es.enter_context(nc.sbuf_tensor("p2c_" + n, s, d))
            Qs = [sb(f"Q{i}", [96, SEQ], BF16) for i in range(2)]
            Ks = [sb(f"K{i}", [96, SEQ], BF16) for i in range(2)]
            Vs = [sb(f"V{i}", [128, NKB, 65], BF16) for i in range(2)]
            LA, NPT, SBANKS = 3, 6, (0, 1, 2, 6, 7)
            pT = [sb(f"pT{i}", [128, TT], BF16) for i in range(NPT)]
            ost = [sb(f"ost{i}", [64, TT], BF16) for i in range(2)]
            rrow = sb("rrow", [65, TT], F32)
            rbc = sb("rbc", [64, TT], F32)
            onesr2 = sb("onesr2", [65, 64], F32)
            S.add("pool", lambda e: e.memset(onesr2[:], 1.0), writes=["onesr2"])
            for i in range(2):
                S.add("pool", lambda e, i=i: e.memset(Vs[i][:, :, 64:65], 1.0), writes=[("Vone", i)])
            pti = 0
            poi = 0
            psi = 0
            pending = []
            import os
            for hi in range(int(os.environ.get('P2HEADS', '8'))):
                fox = hi < 4
                h = hi % 4
                sl = hi % 2
                R = 65 if fox else 96
                scale = 0.125 if fox else 96 ** -0.5
                ramp = rampF if fox else rampM
                rres = "rampF" if fox else "rampM"
                if fox:
                    for hf in range(2):
                        S.add("sp", lambda e, sl=sl, hf=hf, h=h: e.dma_start(out=Qs[sl][0:64, hf * HALF:(hf + 1) * HALF], in_=I["fqT2"][hf, h * 64:(h + 1) * 64, :]),
                              reads=IR.get("fqT2", []), writes=[("Q", sl, hf)], dma=True, chan=("Q", sl))
                        S.add("sp", lambda e, sl=sl, hf=hf, h=h: e.dma_start(out=Ks[sl][0:64, hf * HALF:(hf + 1) * HALF], in_=I["fkT2"][hf, h * 64:(h + 1) * 64, :]),
                              reads=IR.get("fkT2", []), writes=[("K", sl, hf)], dma=True, chan=("K", sl))
                        S.add("sp", lambda e, sl=sl, hf=hf, h=h: e.dma_start(out=Vs[sl][:, hf * 32:(hf + 1) * 32, 0:64],
                                                                             in_=I["fv2"][hf, :, h * 64:(h + 1) * 64].rearrange("(b p) d -> p b d", p=128)),
                              reads=IR.get("fv2", []), writes=[("V", sl, hf)], dma=True, chan=("V", sl))
                    S.add("sp", lambda e, sl=sl, h=h: e.dma_start(out=Qs[sl][64:65, :], in_=scr["caug"][h:h + 1, :]), reads=["caug"], writes=[("Q", sl, 2)], dma=True, chan=("Q", sl))
                    S.add("pool", lambda e, sl=sl: e.memset(Ks[sl][64:65, :], 1.0), writes=[("K", sl, 2)])
                else:
                    S.add("sp", lambda e, sl=sl, h=h: e.dma_start(out=Qs[sl][:], in_=scr["QT"][h]),
                          reads=[("QT", h, tc) for tc in range(SEQ // TT)], writes=[("Q", sl, 0), ("Q", sl, 1), ("Q", sl, 2)], dma=True, chan=("Q", sl))
                    S.add("sp", lambda e, sl=sl, h=h: e.dma_start(out=Ks[sl][:], in_=scr["KT"][h]),
                          reads=[("KTr", h, tc) for tc in range(SEQ // TT)] + [("KTn", h, tc) for tc in range(SEQ // TT)],
                          writes=[("K", sl, 0), ("K", sl, 1), ("K", sl, 2)], dma=True, chan=("K", sl))
                    S.add("sp", lambda e, sl=sl, h=h: e.dma_start(out=Vs[sl][:, :, 0:64], in_=scr["Vd"][h].rearrange("(b p) d -> p b d", p=128)),
                          reads=[("Vd", q) for q in range(NKB)], writes=[("V", sl, 0), ("V", sl, 1)], dma=True, chan=("V", sl))
                qres = [("Q", sl, i) for i in range(3)]
                kres = [("K", sl, i) for i in range(3)]
                vres = [("V", sl, 0), ("V", sl, 1), ("Vone", sl)]
                for i in range(SEQ // TT):
                    if hi == 4 and i == 1 and hook_mid is not None:
                        hook_mid()
                    nkb = 4 * i + 4
                    po = 3 + (poi % 2)
                    poi += 1
                    ptof = {}
                    for step in range(nkb + LA):
                        if step < nkb:
                            j = step
                            ps = SBANKS[psi % len(SBANKS)]
                            psi += 1
                            diag = j >= 4 * i
                            S.add("pe", lambda e, sl=sl, ps=ps, i=i, j=j, R=R, diag=diag: e.matmul(
                                psum[ps][:], lhsT=Ks[sl][0:R, j * 128:(j + 1) * 128], rhs=Qs[sl][0:R, i * TT:(i + 1) * TT], start=True, stop=not diag),
                                reads=qres + kres, writes=[("ps", ps)])
                            if diag:
                                off = (j - 4 * i) * 128
                                S.add("pe", lambda e, ps=ps, off=off, ramp=ramp: e.matmul(
                                    psum[ps][:], lhsT=identb[:], rhs=ramp[:, 384 - off:384 - off + TT], start=False, stop=True),
                                    reads=["identb", rres], writes=[("ps", ps)])
                            pt = pti % NPT
                            pti += 1
                            ptof[j] = pt
                            if fox:
                                S.add("act", lambda e, ps=ps, pt=pt, j=j, h=h, scale=scale: e.activation(
                                    out=pT[pt][:], in_=psum[ps][:], func=AF.Exp, bias=ncumT[:, j * 4 + h:j * 4 + h + 1], scale=scale),
                                    reads=[("ps", ps), "ncumT"], writes=[("pT", pt)])
                            else:
                                S.add("act", lambda e, ps=ps, pt=pt, scale=scale: e.activation(out=pT[pt][:], in_=psum[ps][:], func=AF.Exp, scale=scale),
                                      reads=[("ps", ps)], writes=[("pT", pt)])
                        if step == LA - 1 and pending:
                            pending.pop(0)()
                        if step >= LA:
                            j = step - LA
                            pt = ptof[j]
                            S.add("pe", lambda e, sl=sl, po=po, pt=pt, j=j, nkb=nkb: e.matmul(
                                psum[po][0:65, :], lhsT=Vs[sl][:, j, :], rhs=pT[pt][:], start=(j == 0), stop=(j == nkb - 1)),
                                reads=vres + [("pT", pt)], writes=[("ps", po)])
                    def epilogue(po=po, hi=hi, i=i, fox=fox, h=h):
                        S.add("dve", lambda e: e.reciprocal(out=rrow[64:65, :], in_=psum[po][64:65, :]), reads=[("ps", po)], writes=["rrow"])
                        S.add("pe", lambda e: e.matmul(psum[5][0:64, :], lhsT=onesr2[64:65, :], rhs=rrow[64:65, :], start=True, stop=True),
                              reads=["rrow", "onesr2"], writes=[("ps", 5)])
                        S.add("act", lambda e: e.copy(out=rbc[:], in_=psum[5][0:64, :]), reads=[("ps", 5)], writes=["rbc"])
                        ob = (hi * 16 + i) % 2
                        S.add("dve", lambda e: e.tensor_tensor(out=ost[ob][:], in0=psum[po][0:64, :], in1=rbc[:], op=ALU.mult),
                              reads=[("ps", po), "rbc"], writes=[("ost", ob)])
                        br = 2 if fox else 1
                        S.add("sp", lambda e: e.dma_start(out=brT_out[br, h * 64:(h + 1) * 64, i * TT:(i + 1) * TT], in_=ost[ob][:]),
                              reads=[("ost", ob)], writes=[("brout", br, h, i)], dma=True, chan=("ost", ob))
                    pending.append(epilogue)
            while pending:
                pending.pop(0)()
        S.barrier()

        if hook_d is not None:
            hook_d()
        with ExitStack() as es:
            sb = lambda n, s, d: es.enter_context(nc.sbuf_tensor("p2d_" + n, s, d))
            u = sb("u", [128, TL + 3], F32)
            ug = sb("ug", [128, TL], F32)
            xc = sb("xc", [128, TL], F32)
            xcb = sb("xcb", [128, TL], BF16)
            rr = sb("rr", [128, TL], F32)
            ig = sb("ig", [128, TL], F32)
            aa = sb("aa", [128, TL], F32)
            s1 = sb("s1", [128, TL], F32)
            bb = sb("bb", [128, TL], F32)
            hh = sb("hh", [128, TL], F32)
            g1 = sb("g1", [128, TL], F32)
            g2 = sb("g2", [128, TL], F32)
            osb = [sb(f"osb{i}", [128, TL], BF16) for i in range(2)]
            wab = sb("wab", [128, 2, 128], BF16)
            cv = sb("cv", [128, 12], F32)
            for ct in range(2):
                c0 = ct * 128
                S.add("pool", lambda e: e.memset(wab[:], 0.0), writes=["wab"])
                for q in range(2):
                    hd = ct * 2 + q
                    S.add("pool", lambda e, q=q, hd=hd: e.dma_start(out=wab[q * 64:(q + 1) * 64, 0, q * 64:(q + 1) * 64], in_=W["wa"][hd]),
                          reads=["wab"], writes=[("wab", 0, q)], dma=True, chan="wab")
                    S.add("pool", lambda e, q=q, hd=hd: e.dma_start(out=wab[q * 64:(q + 1) * 64, 1, q * 64:(q + 1) * 64], in_=W["wx"][hd]),
                          reads=["wab"], writes=[("wab", 1, q)], dma=True, chan="wab")
                wabres = ["wab"] + [("wab", i, q) for i in range(2) for q in range(2)]
                S.add("sp", lambda e, c0=c0: e.dma_start(out=cv[:, 0:4], in_=W["conv_w"][:, c0:c0 + 128].rearrange("k p -> p k"), allow_slow_non_contiguous=True),
                      writes=[("cv", 0)], dma=True, chan="cv")
                for ci, nm in ((4, "conv_b"), (5, "ba"), (6, "bx"), (7, "lam")):
                    S.add("sp", lambda e, ci=ci, nm=nm, c0=c0: e.dma_start(out=cv[:, ci:ci + 1], in_=W[nm][c0:c0 + 128].rearrange("(p o) -> p o", o=1)),
                          writes=[("cv", ci)], dma=True, chan="cv")
                S.add("act", lambda e: e.activation(out=cv[:, 8:9], in_=cv[:, 7:8], func=AF.Exp, scale=-1.0), reads=[("cv", 7)], writes=[("cv", 8)])
                S.add("act", lambda e: e.activation(out=cv[:, 8:9], in_=cv[:, 8:9], func=AF.Ln, bias=1.0, scale=1.0), reads=[("cv", 8)], writes=[("cv", 8)])
                S.add("dve", lambda e: e.tensor_scalar_mul(out=cv[:, 9:10], in0=cv[:, 8:9], scalar1=-16.0), reads=[("cv", 8)], writes=[("cv", 9)])
                S.add("dve", lambda e: e.tensor_scalar_mul(out=cv[:, 8:9], in0=cv[:, 8:9], scalar1=-8.0), reads=[("cv", 8), ("cv", 9)], writes=[("cv", 8)])
                cvall = [("cv", i) for i in range(10)]
                for c in range(SEQ // TL):
                    t0 = c * TL
                    S.add("sp", lambda e, c0=c0, t0=t0: e.dma_start(out=u[:, 3:], in_=hsl(I["uT2"], c0, c0 + 128, t0, TL)), reads=IR.get("uT2", []), writes=[("u", 1)], dma=True, chan="u")
                    if c == 0:
                        S.add("pool", lambda e: e.memset(u[:, 0:3], 0.0), writes=[("u", 0)])
                    else:
                        S.add("sp", lambda e, c0=c0, t0=t0: e.dma_start(out=u[:, 0:3], in_=hsl(I["uT2"], c0, c0 + 128, t0 - 3, 3)), reads=IR.get("uT2", []), writes=[("u", 0)], dma=True, chan="u0")
                    S.add("sp", lambda e, c0=c0, t0=t0: e.dma_start(out=ug[:], in_=hsl(I["uT2"], 256 + c0, 256 + c0 + 128, t0, TL)), reads=IR.get("uT2", []), writes=["ug"], dma=True, chan="ug")
                    ures = [("u", 0), ("u", 1)]
                    S.add("dve", lambda e: e.tensor_scalar(out=xc[:], in0=u[:, 0:TL], scalar1=cv[:, 0:1], scalar2=cv[:, 4:5], op0=ALU.mult, op1=ALU.add),
                          reads=ures + cvall, writes=["xc"])
                    for k in range(1, 4):
                        S.add("dve", lambda e, k=k: e.scalar_tensor_tensor(out=xc[:], in0=u[:, k:k + TL], scalar=cv[:, k:k + 1], in1=xc[:], op0=ALU.mult, op1=ALU.add),
                              reads=ures + cvall + ["xc"], writes=["xc"])
                    S.add("pool", lambda e: e.tensor_copy(out=xcb[:], in_=xc[:]), reads=["xc"], writes=["xcb"])
                    for sbi in range(TL // TT):
                        ssl = slice(sbi * TT, (sbi + 1) * TT)
                        pr = 6 + (sbi % 2)
                        S.add("pe", lambda e, pr=pr, ssl=ssl: e.matmul(psum[pr][:], lhsT=wab[:, 0, :], rhs=xcb[:, ssl], start=True, stop=True),
                              reads=wabres + ["xcb"], writes=[("ps", pr)])
                        S.add("act", lambda e, pr=pr, ssl=ssl: e.activation(out=rr[:, ssl], in_=psum[pr][:], func=AF.Sigmoid, bias=cv[:, 5:6], scale=1.0),
                              reads=[("ps", pr)] + cvall, writes=[("rr", sbi)])
                        pr2 = 1 + (sbi % 2)
                        S.add("pe", lambda e, pr2=pr2, ssl=ssl: e.matmul(psum[pr2][:], lhsT=wab[:, 1, :], rhs=xcb[:, ssl], start=True, stop=True),
                              reads=wabres + ["xcb"], writes=[("ps", pr2)])
                        S.add("act", lambda e, pr2=pr2, ssl=ssl: e.activation(out=ig[:, ssl], in_=psum[pr2][:], func=AF.Sigmoid, bias=cv[:, 6:7], scale=1.0),
                              reads=[("ps", pr2)] + cvall, writes=[("ig", sbi)])
                    rrres = [("rr", i) for i in range(TL // TT)]
                    igres = [("ig", i) for i in range(TL // TT)]
                    S.add("act", lambda e: e.activation(out=aa[:], in_=rr[:], func=AF.Exp, scale=cv[:, 8:9]), reads=rrres + cvall, writes=["aa"])
                    S.add("act", lambda e: e.activation(out=s1[:], in_=rr[:], func=AF.Exp, scale=cv[:, 9:10]), reads=rrres + cvall, writes=["s1"])
                    S.add("dve", lambda e: e.tensor_scalar_min(out=s1[:], in0=s1[:], scalar1=1.0), reads=["s1"], writes=["s1"])
                    S.add("act", lambda e: e.activation(out=s1[:], in_=s1[:], func=AF.Sqrt, bias=1.0, scale=-1.0), reads=["s1"], writes=["s1"])
                    S.add("dve", lambda e: e.tensor_tensor(out=bb[:], in0=ig[:], in1=xc[:], op=ALU.mult), reads=igres + ["xc"], writes=["bb"])
                    S.add("dve", lambda e: e.tensor_tensor(out=bb[:], in0=bb[:], in1=s1[:], op=ALU.mult), reads=["bb", "s1"], writes=["bb"])
                    if c == 0:
                        S.add("dve", lambda e: e.tensor_tensor_scan(out=hh[:], data0=aa[:], data1=bb[:], initial=0.0, op0=ALU.mult, op1=ALU.add),
                              reads=["aa", "bb"], writes=["hh"])
                    else:
                        S.add("dve", lambda e: e.tensor_tensor_scan(out=hh[:], data0=aa[:], data1=bb[:], initial=cv[:, 10:11], op0=ALU.mult, op1=ALU.add),
                              reads=["aa", "bb", "carry"], writes=["hh"])
                    S.add("dve", lambda e: e.tensor_copy(out=cv[:, 10:11], in_=hh[:, TL - 1:TL]), reads=["hh"], writes=["carry"])
                    S.add("pool", lambda e: e.tensor_tensor(out=g1[:], in0=ug[:], in1=ug[:], op=ALU.mult), reads=["ug"], writes=["g1"])
                    S.add("pool", lambda e: e.tensor_scalar(out=g1[:], in0=g1[:], scalar1=0.044715, scalar2=1.0, op0=ALU.mult, op1=ALU.add), reads=["g1"], writes=["g1"])
                    S.add("pool", lambda e: e.tensor_tensor(out=g1[:], in0=g1[:], in1=ug[:], op=ALU.mult), reads=["g1", "ug"], writes=["g1"])
                    S.add("act", lambda e: e.activation(out=g2[:], in_=g1[:], func=AF.Sigmoid, scale=1.5957691216057308), reads=["g1"], writes=["g2"])
                    S.add("pool", lambda e: e.tensor_tensor(out=g2[:], in0=g2[:], in1=ug[:], op=ALU.mult), reads=["g2", "ug"], writes=["g2"])
                    ob = (ct * 4 + c) % 2
                    S.add("dve", lambda e, ob=ob: e.tensor_tensor(out=osb[ob][:], in0=hh[:], in1=g2[:], op=ALU.mult), reads=["hh", "g2"], writes=[("osb", ob)])
                    S.add("sp", lambda e, ob=ob, c0=c0, t0=t0: e.dma_start(out=brT_out[0, c0:c0 + 128, t0:t0 + TL], in_=osb[ob][:]),
                          reads=[("osb", ob)], writes=[("brout", 0, ct, c)], dma=True, chan=("osb", ob))
    S.barrier()


PAIRS = [[0, 1], [2, 3], [4, 5], [6, 7]]


def _gather_chunks(S, nc, name, src, rc, rd, chan="cc"):
    R, N = src.shape
    nch = R // rc
    assert nch * rc == R
    G = nc.dram_tensor(name, [nch, 2 * rc, N], src.dtype).ap()
    for i in range(nch):
        S.add("pool", lambda e, i=i: e.collective_compute("AllGather", ALU.bypass, replica_groups=PAIRS, ins=[src[i * rc:(i + 1) * rc, :]], outs=[G[i]]),
              reads=rd, writes=[("G", name)], dma=True, chan=chan, inc=1)
    return G.rearrange("c (h r) n -> c h r n", h=2)


def exchange1(S, nc, T, p1o, gsel, dynq="sp"):
    dt_ = lambda n, s, d=F32: nc.dram_tensor(T + n, s, d).ap()
    I = dict(uT2=dt_("i_uT2", [2, 512, NTOK]), fqT2=dt_("i_fqT2", [2, 256, NTOK], BF16), fkT2=dt_("i_fkT2", [2, 256, NTOK], BF16),
             fv2=dt_("i_fv2", [2, NTOK, 256], BF16), flT2=dt_("i_flT2", [2, 4, NTOK]))
    Gl = _gather_chunks(S, nc, T + "g_flT", p1o["flT"], 8, [("s1", "flT")], chan="cc0").rearrange("c h (q r) n -> c h q r n", q=2)
    S.add(dynq, lambda e: e.dma_start(out=I["flT2"].rearrange("h (o r) n -> h o r n", o=1), in_=Gl[0, :, bass.ds(gsel(e)[0], 1)]),
          reads=[("G", T + "g_flT")], writes=[("I", "flT2")], dma=True, chan=("rg", dynq, 0))
    I["cq_chunks"] = _gather_chunks(S, nc, T + "g_cqT", p1o["cqT"], 128, [("s1", "cqT")], chan="cc0")
    I["ckv_chunks"] = _gather_chunks(S, nc, T + "g_ckvT", p1o["ckvT"], 128, [("s1", "ckvT")], chan="cc0")
    I["cq_res"] = ("G", T + "g_cqT")
    I["ckv_res"] = ("G", T + "g_ckvT")
    Gq = _gather_chunks(S, nc, T + "g_fqT", p1o["fqT"], 256, [("s1", "fqT")], chan="cc1")
    Gk = _gather_chunks(S, nc, T + "g_fkT", p1o["fkT"], 256, [("s1", "fkT")], chan="cc1")
    Gv = _gather_chunks(S, nc, T + "g_fv", p1o["fv"], 2048, [("s1", "fv")], chan="cc1").rearrange("c h t (q d) -> c h t q d", q=2)
    Gu = _gather_chunks(S, nc, T + "g_uT", p1o["uT"], 128, [("s1", "uT")], chan="cc1").rearrange("(q c) h r n -> q c h r n", c=2)

    def late():
        fns = []
        fns.append((lambda e: e.dma_start(out=I["fqT2"].unsqueeze(0), in_=Gq[bass.ds(gsel(e)[0], 1)]), "g_fqT", "fqT2"))
        fns.append((lambda e: e.dma_start(out=I["fkT2"].unsqueeze(0), in_=Gk[bass.ds(gsel(e)[0], 1)]), "g_fkT", "fkT2"))
        for hf in range(2):
            for c in range(2):
                fns.append((lambda e, hf=hf, c=c: e.dma_start(out=I["fv2"][hf, c * 2048:(c + 1) * 2048, :].rearrange("t (o d) -> t o d", o=1),
                                                              in_=Gv[c, hf, :, bass.ds(gsel(e)[0], 1), :]), "g_fv", "fv2"))
        for hf in range(2):
            fns.append((lambda e, hf=hf: e.dma_start(out=I["uT2"][hf:hf + 1, 0:256, :].rearrange("o (c r) n -> o c r n", c=2), in_=Gu[0:2][bass.ds(gsel(e)[0], 1), :, hf]), "g_uT", "uT2"))
            fns.append((lambda e, hf=hf: e.dma_start(out=I["uT2"][hf:hf + 1, 256:512, :].rearrange("o (c r) n -> o c r n", c=2), in_=Gu[2:4][bass.ds(gsel(e)[0], 1), :, hf]), "g_uT", "uT2"))
        for i, (fn, src, dst) in enumerate(fns):
            S.add(dynq, fn, reads=[("G", T + src)], writes=[("I", dst, i)], dma=True, chan=("rg", dynq, 1))
    I["res"] = dict(flT2=[("I", "flT2")], fqT2=[("I", "fqT2", 0)], fkT2=[("I", "fkT2", 1)], fv2=[("I", "fv2", i) for i in range(2, 6)],
                    uT2=[("I", "uT2", i) for i in range(6, 10)])
    return I, late


class Exchange2:
    def __init__(self, S, nc, T, br_mine):
        self.S, self.nc, self.T = S, nc, T
        self.src = br_mine.rearrange("j r n -> (j r) n")
        self.G = nc.dram_tensor(T + "g_br", [6, 256, 8192], BF16).ap()

    def gather(self, j, reads):
        for rh in range(2):
            c = j * 2 + rh
            self.S.add("pool", lambda e, c=c: e.collective_compute("AllGather", ALU.bypass, replica_groups=PAIRS,
                                                                     ins=[self.src[c * 128:(c + 1) * 128, :]], outs=[self.G[c]]),
                       reads=reads, writes=[("Gbr", j, rh)], dma=True, chan="cc2", inc=1)

    def regroup(self, gsel, dynq):
        S, T = self.S, self.T
        br_in = self.nc.dram_tensor(T + "br_in", [3, 512, NTOK], BF16).ap()
        Gb = self.G.rearrange("(j rh) (g r) (h n) -> j rh g r h n", j=3, g=2, h=2)
        for j in range(3):
            for gp in range(2):
                S.add(dynq, lambda e, j=j, gp=gp: e.dma_start(out=br_in[j, gp * 256:(gp + 1) * 256, :].rearrange("(rh r) (o n) -> rh r o n", rh=2, o=1),
                                                             in_=Gb[j, :, gp, :, bass.ds(gsel(e)[0], 1), :]),
                      reads=[("Gbr", j, 0), ("Gbr", j, 1)], writes=[("brin", j, gp)], dma=True, chan=("rg", dynq))
        return br_in


def make_consts():
    ident = np.eye(128, dtype=np.float32)
    p = np.arange(128)[:, None]; g = np.arange(896)[None, :]
    ramp_fox = np.where(g - p >= 384, 0.0, -30000.0).astype(np.float32)
    ramp_mla = np.where((g // 64) >= (p // 64) + 6, 0.0, -30000.0).astype(np.float32)
    pos = np.arange(8192, dtype=np.float32)
    inv_freq = (np.float32(10000.0) ** (-np.arange(0, 32, 2, dtype=np.float32) / np.float32(32))).astype(np.float32)
    ang = (pos[:, None] * inv_freq[None, :]).astype(np.float32)
    cos = np.cos(ang).astype(np.float32).T; sin = np.sin(ang).astype(np.float32).T
    cosT = np.ascontiguousarray(np.concatenate([cos, cos], 0))
    sinS = np.ascontiguousarray(np.concatenate([-sin, sin], 0))
    return dict(ident=ident, ramp_fox=ramp_fox, ramp_mla=ramp_mla, cosT=cosT, sinS=sinS)


_BF = ml_dtypes.bfloat16
_PROGS = {}
DEPTH = 2
XQ = ["sp", "pool"]


class NcTag:
    def __init__(self, nc, tag):
        self._nc = nc
        self._tag = tag

    def sbuf_tensor(self, name, shape, dt):
        return self._nc.sbuf_tensor(self._tag + name, shape, dt)

    def psum_tensor(self, name, shape, dt):
        return self._nc.psum_tensor(self._tag + name, shape, dt)

    def __getattr__(self, a):
        return getattr(self._nc, a)


def _di(nc, n, s, d=F32):
    return nc.dram_tensor(n, s, d, kind="ExternalInput").ap()


def build_fused(stage=9):
    nc = bass.Bass("TRN2", target_bir_lowering=False)
    xT = _di(nc, "xT", [1024, NTOK])
    pT = _di(nc, "pT", [DEPTH, 256, NTOK])
    Wf = dict(w_in=_di(nc, "w_in", [DEPTH, 1024, D_IN]), gate_b=_di(nc, "gate_b", [DEPTH, 3072]), mixg=_di(nc, "mixg", [DEPTH, 1024]),
              ffng=_di(nc, "ffng", [DEPTH, 1024]), pleg=_di(nc, "pleg", [DEPTH, 1024]), w_o=_di(nc, "w_o", [DEPTH, 1024, 1024]),
              w_gu=_di(nc, "w_gu", [DEPTH, 1024, 5632]), w_down=_di(nc, "w_down", [DEPTH, 2816, 1024]), w_pg=_di(nc, "w_pg", [DEPTH, 1024, 1024]),
              w_ple=_di(nc, "w_ple", [DEPTH, 256, 1024]), w_br0=_di(nc, "w_br0", [DEPTH, 512, 1024]), w_br1=_di(nc, "w_br1", [DEPTH, 512, 1024]),
              w_br2=_di(nc, "w_br2", [DEPTH, 512, 1024]), fing=_di(nc, "fing", [1024]))
    Wg = dict(conv_w=_di(nc, "conv_w", [DEPTH, 4, 256]), conv_b=_di(nc, "conv_b", [DEPTH, 256]), wa=_di(nc, "wa", [DEPTH, 4, 64, 64]), ba=_di(nc, "ba", [DEPTH, 256]),
              wx=_di(nc, "wx", [DEPTH, 4, 64, 64]), bx=_di(nc, "bx", [DEPTH, 256]), lam=_di(nc, "lam", [DEPTH, 256]), qn=_di(nc, "qn", [DEPTH, 384]),
              wuq=_di(nc, "wuq", [DEPTH, 384, 384]), kvn=_di(nc, "kvn", [DEPTH, 256]), wukv=_di(nc, "wukv", [DEPTH, 256, 512]), bf=_di(nc, "bf", [DEPTH, 4]))
    C = dict(ident=_di(nc, "ident", [128, 128]), ramp_fox=_di(nc, "ramp_fox", [128, 896]), ramp_mla=_di(nc, "ramp_mla", [128, 896]),
             cosT=_di(nc, "cosT", [32, 8192]), sinS=_di(nc, "sinS", [32, 8192]))
    out = nc.dram_tensor("out", [1024, NTOK], F32, kind="ExternalOutput").ap()
    dt_ = lambda n, s, d=F32: nc.dram_tensor(n, s, d).ap()
    S = Sched(nc)
    xres = xT
    _pid = {}

    def gsel(e):
        if id(e) not in _pid:
            _pid[id(e)] = (e.partition_id() % 2,)
        return _pid[id(e)]
    for l in range(DEPTH):
        T = f"L{l}_"
        nct = NcTag(nc, T)
        p1o = {nm: dt_(T + "s1_" + nm, shp, d) for nm, shp, d in
               [("uT", [1024, NTOK], F32), ("cqT", [384, NTOK], F32), ("ckvT", [384, NTOK], F32), ("fqT", [512, NTOK], BF16),
                ("fkT", [512, NTOK], BF16), ("fv", [NTOK, 512], BF16), ("flT", [8, NTOK], F32)]}
        phase1(S, nct, xres, Wf["w_in"][l], Wf["mixg"][l], p1o)
        I, late1 = exchange1(S, nc, T, p1o, gsel, dynq=XQ[l])
        if stage == 1:
            S.add("sp", lambda e, I=I: e.dma_start(out=out[0:4, :], in_=I["flT2"][0]), writes=["dummy"], dma=True, chan="dummy")
            break
        Wl = {k: v[l] for k, v in Wg.items()}
        br_mine = dt_(T + "br_mine", [3, 256, 8192], BF16)
        scr = dict(QT=dt_(T + "sQT", [4, 96, 8192], BF16), KT=dt_(T + "sKT", [4, 96, 8192], BF16),
                   Vd=dt_(T + "sVd", [4, 8192, 64], BF16), caug=dt_(T + "scaug", [4, 8192], BF16))
        wb = {}

        def precast(l=l, T=T, wb=wb):
            srcs = dict(w_o=Wf["w_o"][l], w_gu=Wf["w_gu"][l], w_down=Wf["w_down"][l], w_pg=Wf["w_pg"][l], w_ple=Wf["w_ple"][l],
                        w_br0=Wf["w_br0"][l], w_br1=Wf["w_br1"][l], w_br2=Wf["w_br2"][l], w_gate=Wf["w_in"][l][:, C_GATE:C_GATE + 3072])
            for nm, src in srcs.items():
                R, Cc = src.shape
                dst = dt_(T + "bf_" + nm, [R, Cc], BF16)
                wb[nm] = dst
                for r0 in range(0, R, 256):
                    r1 = min(R, r0 + 256)
                    S.add("pool", lambda e, src=src, dst=dst, r0=r0, r1=r1: e.dma_start(out=dst[r0:r1, :], in_=src[r0:r1, :]),
                          writes=[("wbf", nm, r0)], dma=True, chan="precast")

        ex2 = Exchange2(S, nc, T, br_mine)
        rows = lambda br: [("brout", br, h, i) for h in range(4) for i in range(16)]
        phase2(S, nct, I, Wl, C, br_mine, scr, hook_c=lambda: (late1(), precast()),
               hook_mid=lambda: ex2.gather(2, rows(2)), hook_d=lambda: ex2.gather(1, rows(1)))
        if stage == 2:
            S.add("sp", lambda e, I=I: e.dma_start(out=out[0:4, :], in_=I["flT2"][0]), writes=["dummy"], dma=True, chan="dummy")
            break
        ex2.gather(0, [("brout", 0, ct, c) for ct in range(2) for c in range(4)])
        br_in = ex2.regroup(gsel, XQ[l])
        if stage == 3:
            S.add("sp", lambda e, I=I: e.dma_start(out=out[0:4, :], in_=I["flT2"][0]), writes=["dummy"], dma=True, chan="dummy")
            break
        final = (l == DEPTH - 1) or stage in (4, 5)
        W3 = dict(w_in=Wf["w_in"][l], gate_b=Wf["gate_b"][l], mixg=Wf["mixg"][l], ffng=Wf["ffng"][l], pleg=Wf["pleg"][l], w_o=wb["w_o"],
                  w_gu=wb["w_gu"], w_down=wb["w_down"], w_pg=wb["w_pg"], w_ple=wb["w_ple"], w_gate=wb["w_gate"],
                  w_br=[wb["w_br0"], wb["w_br1"], wb["w_br2"]])
        if final:
            W3["fing"] = Wf["fing"]
            phase3(S, nct, xres, out, br_in, pT[l], W3, final_out=out, wq="sp")
            if stage in (4, 5):
                break
        else:
            xnext = dt_(T + "xnext", [1024, NTOK])
            phase3(S, nct, xres, xnext, br_in, pT[l], W3, wq="sp")
            xres = xnext
    S.emit()
    return nc


def kernel(x, p, mix_norm, w_in, gate_b, conv_w, conv_b, lru_wa, lru_ba, lru_wx, lru_bx, lru_lambda,
           mla_q_norm, mla_wuq, mla_kv_norm, mla_wukv, fox_bf, w_br_a, w_br_b, w_br_c, w_o,
           ffn_norm, w_gate_up, w_down, ple_norm, w_ple_gate, w_ple, final_norm):
    A = lambda a: np.ascontiguousarray(np.asarray(a))
    x = np.asarray(x)
    p = np.asarray(p)
    cores = list(range(8))
    Cn = make_consts()
    if "fused" not in _PROGS:
        import os
        _PROGS["fused"] = build_fused(int(os.environ.get("FSTAGE", "9")))
    nc = _PROGS["fused"]
    full = dict(w_in=A(w_in), gate_b=A(gate_b), mixg=A(mix_norm), ffng=A(ffn_norm), pleg=A(ple_norm), w_o=A(w_o), w_gu=A(w_gate_up),
                w_down=A(w_down), w_pg=A(w_ple_gate), w_ple=A(w_ple), w_br0=A(w_br_a), w_br1=A(w_br_b), w_br2=A(w_br_c), fing=A(final_norm),
                qn=A(mla_q_norm), kvn=A(mla_kv_norm))
    full.update(Cn)
    in_maps = []
    for c in cores:
        b, g = c // 2, c % 2
        m = dict(full)
        m["xT"] = A(x[b, g * NTOK:(g + 1) * NTOK].T)
        m["pT"] = A(np.transpose(p[:, b, g * NTOK:(g + 1) * NTOK], (0, 2, 1)))
        m.update(conv_w=A(np.asarray(conv_w)[:, :, g * 256:(g + 1) * 256]), conv_b=A(np.asarray(conv_b)[:, g * 256:(g + 1) * 256]),
                 wa=A(np.asarray(lru_wa)[:, g * 4:(g + 1) * 4]), ba=A(np.asarray(lru_ba)[:, g * 256:(g + 1) * 256]),
                 wx=A(np.asarray(lru_wx)[:, g * 4:(g + 1) * 4]), bx=A(np.asarray(lru_bx)[:, g * 256:(g + 1) * 256]),
                 lam=A(np.asarray(lru_lambda)[:, g * 256:(g + 1) * 256]), wuq=A(np.asarray(mla_wuq)[:, :, g * 384:(g + 1) * 384]),
                 wukv=A(np.asarray(mla_wukv)[:, :, g * 512:(g + 1) * 512]), bf=A(np.asarray(fox_bf)[:, g * 4:(g + 1) * 4]))
        in_maps.append(m)
    res = run_bass_kernel_spmd(nc, in_maps, core_ids=cores).results
    out = np.empty((4, 8192, 1024), np.float32)
    for c in cores:
        out[c // 2, (c % 2) * NTOK:(c % 2 + 1) * NTOK] = np.asarray(res[c]["out"]).T
    return out
```
